# Optimizing a Trainium2 kernel written in Bass

```python
import jax, jax.numpy as jnp
from jax import lax
import numpy as np

D_MODEL = 1024
BATCH = 16
SEQ = 2048
DEPTH = 2

GRID_W = 64
HEAD_DIM = 64
A_HEADS = 8
A_KV_HEADS = 2
A_GROUPS = A_HEADS // A_KV_HEADS
Q_BLOCK = 128
ROPE_THETA = 10000.0
POOL_WINDOWS = (2, 4, 8, 16)
POOL_GROUP = 128
POOL_WIDTH = POOL_GROUP * len(POOL_WINDOWS)
C_HEADS = 8
NA_ROWS = 8
NA_COLS = 16
A_Q = A_HEADS * HEAD_DIM
A_KV = A_KV_HEADS * HEAD_DIM
C_W = C_HEADS * HEAD_DIM
N_BRANCH = 3
BRANCH_WIDTH = 512
IN_WIDTHS = (A_Q, A_KV, A_KV, POOL_WIDTH, C_W, C_W, C_W, N_BRANCH * D_MODEL)
D_IN = sum(IN_WIDTHS)
D_FF = 2816
CONV_W = 3
EPS = 1e-6

kernel_name = "hybrid_gated_axial_pool_natten_encoder"


def rmsnorm(x, g):
    xf = x.astype(jnp.float32)
    y = xf * lax.rsqrt(jnp.mean(xf * xf, axis=-1, keepdims=True) + EPS)
    return (y * g.astype(jnp.float32)).astype(x.dtype)


def rope_axis(x, pos):
    dim = x.shape[-1]
    half = dim // 2
    freqs = ROPE_THETA ** (-jnp.arange(0, dim, 2, dtype=jnp.float32) / dim)
    ang = pos.astype(jnp.float32)[:, None] * freqs[None, :]
    cos = jnp.cos(ang)[:, None, :]
    sin = jnp.sin(ang)[:, None, :]
    xf = x.astype(jnp.float32)
    x1, x2 = xf[..., :half], xf[..., half:]
    out = jnp.concatenate([x1 * cos - x2 * sin, x1 * sin + x2 * cos], axis=-1)
    return out.astype(x.dtype)


def rope_2d(x, row, col):
    half = x.shape[-1] // 2
    return jnp.concatenate([rope_axis(x[..., :half], row), rope_axis(x[..., half:], col)], axis=-1)


def global_axial_attention(q, k, v, gq, gk):
    B, S = q.shape[0], q.shape[1]
    t = jnp.arange(S)
    row, col = t // GRID_W, t % GRID_W
    q = rmsnorm(q.reshape(B, S, A_HEADS, HEAD_DIM), gq)
    k = rmsnorm(k.reshape(B, S, A_KV_HEADS, HEAD_DIM), gk)
    v = v.reshape(B, S, A_KV_HEADS, HEAD_DIM)
    q = rope_2d(q, row, col) * (HEAD_DIM ** -0.5)
    k = rope_2d(k, row, col)
    nb = S // Q_BLOCK
    qb = q.reshape(B, nb, Q_BLOCK, A_KV_HEADS, A_GROUPS, HEAD_DIM).transpose(1, 0, 2, 3, 4, 5)

    def block(qi):
        s = jnp.einsum('bqkgd,bskd->bkgqs', qi, k).astype(jnp.float32)
        p = jax.nn.softmax(s, axis=-1)
        return jnp.einsum('bkgqs,bskd->bqkgd', p.astype(v.dtype), v)

    o = lax.map(block, qb)
    return o.transpose(1, 0, 2, 3, 4, 5).reshape(B, S, A_Q)


def pool_mixer(u, w_pool, pool_scale):
    B, S = u.shape[0], u.shape[1]
    uf = u.astype(jnp.float32)
    cs = jnp.concatenate([jnp.zeros((B, 1, POOL_WIDTH), jnp.float32), jnp.cumsum(uf, axis=1)], axis=1)
    t = jnp.arange(S)
    outs = []
    for g, w in enumerate(POOL_WINDOWS):
        sl = slice(g * POOL_GROUP, (g + 1) * POOL_GROUP)
        lo = jnp.clip(t - w // 2, 0, S)
        hi = jnp.clip(t + w - w // 2, 0, S)
        csg = cs[..., sl]
        mean = (csg[:, hi] - csg[:, lo]) / (hi - lo).astype(jnp.float32)[None, :, None]
        diff = (mean - uf[..., sl]).astype(u.dtype)
        outs.append(jnp.einsum('bsc,cd->bsd', diff, w_pool[g]))
    return jnp.concatenate(outs, axis=-1) * pool_scale


def neighbourhood_attention(q, k, v, rpb):
    B, S = q.shape[0], q.shape[1]
    rows = S // GRID_W
    win_r = min(NA_ROWS, rows)
    qg = q.reshape(B, rows, GRID_W, C_HEADS, HEAD_DIM) * (HEAD_DIM ** -0.5)
    kg = k.reshape(B, rows, GRID_W, C_HEADS, HEAD_DIM)
    vg = v.reshape(B, rows, GRID_W, C_HEADS, HEAD_DIM)
    col = jnp.arange(GRID_W)
    col_start = jnp.clip(col - NA_COLS // 2, 0, GRID_W - NA_COLS)
    col_idx = col_start[:, None] + jnp.arange(NA_COLS)[None, :]
    dc = col_idx - col[:, None] + (NA_COLS - 1)

    def row_block(args):
        r, q_row = args
        rs = jnp.clip(r - win_r // 2, 0, rows - win_r)
        k_rows = lax.dynamic_slice_in_dim(kg, rs, win_r, axis=1)
        v_rows = lax.dynamic_slice_in_dim(vg, rs, win_r, axis=1)
        k_nb = k_rows[:, :, col_idx]
        v_nb = v_rows[:, :, col_idx]
        dr = rs + jnp.arange(win_r) - r + (NA_ROWS - 1)
        bias = rpb[:, dr[None, :, None], dc[:, None, :]]
        s = jnp.einsum('bqhd,brqjhd->bhqrj', q_row, k_nb).astype(jnp.float32)
        s = s + bias.astype(jnp.float32)[None]
        p = jax.nn.softmax(s.reshape(B, C_HEADS, GRID_W, win_r * NA_COLS), axis=-1)
        p = p.reshape(B, C_HEADS, GRID_W, win_r, NA_COLS).astype(v.dtype)
        return jnp.einsum('bhqrj,brqjhd->bqhd', p, v_nb)

    o = lax.map(row_block, (jnp.arange(rows), qg.transpose(1, 0, 2, 3, 4)))
    return o.transpose(1, 0, 2, 3, 4).reshape(B, S, C_W)


def conv_gated_mlp(h, w_up, conv_w, conv_b, w_down):
    up = jnp.einsum('bsd,df->bsf', h, w_up)
    pad = jnp.pad(up, ((0, 0), (1, 1), (0, 0)))
    up = pad[:, :-2] * conv_w[0] + pad[:, 1:-1] * conv_w[1] + pad[:, 2:] * conv_w[2] + conv_b
    val, gate = up[..., :D_FF], up[..., D_FF:]
    return jnp.einsum('bsf,fd->bsd', jax.nn.gelu(gate, approximate=True) * val, w_down)


def setup_inputs(seed: int = 0) -> dict:
    key = jax.random.key(seed)
    ks = jax.random.split(key, 20)
    n = lambda k, shape, s: jax.random.normal(k, shape, jnp.float32) * s
    L, D = DEPTH, D_MODEL
    return {
        "x": n(ks[0], (BATCH, SEQ, D), 1.0),
        "norm_mix_pre": 1.0 + n(ks[1], (L, D), 0.1),
        "norm_mix_post": 1.0 + n(ks[2], (L, D), 0.1),
        "norm_ffn_pre": 1.0 + n(ks[3], (L, D), 0.1),
        "norm_ffn_post": 1.0 + n(ks[4], (L, D), 0.1),
        "w_in": n(ks[5], (L, D, D_IN), D ** -0.5),
        "b_gate": n(ks[6], (L, N_BRANCH * D), 0.1),
        "qk_norm_q": 1.0 + n(ks[7], (L, HEAD_DIM), 0.1),
        "qk_norm_k": 1.0 + n(ks[8], (L, HEAD_DIM), 0.1),
        "w_pool": n(ks[9], (L, len(POOL_WINDOWS), POOL_GROUP, POOL_GROUP), POOL_GROUP ** -0.5),
        "pool_scale": 1.0 + n(ks[10], (L, POOL_WIDTH), 0.1),
        "rpb": n(ks[11], (L, C_HEADS, 2 * NA_ROWS - 1, 2 * NA_COLS - 1), 0.5),
        "w_branch": n(ks[12], (L, N_BRANCH, BRANCH_WIDTH, D), BRANCH_WIDTH ** -0.5),
        "w_out": n(ks[13], (L, D, D), D ** -0.5),
        "w_up": n(ks[14], (L, D, 2 * D_FF), D ** -0.5),
        "conv_w": n(ks[15], (L, CONV_W, 2 * D_FF), CONV_W ** -0.5),
        "conv_b": n(ks[16], (L, 2 * D_FF), 0.01),
        "w_down": n(ks[17], (L, D_FF, D), D_FF ** -0.5),
    }


def reference(x, norm_mix_pre, norm_mix_post, norm_ffn_pre, norm_ffn_post, w_in, b_gate,
              qk_norm_q, qk_norm_k, w_pool, pool_scale, rpb, w_branch, w_out,
              w_up, conv_w, conv_b, w_down):
    B, S = x.shape[0], x.shape[1]
    splits = [int(s) for s in np.cumsum(IN_WIDTHS)[:-1]]
    for l in range(DEPTH):
        h = rmsnorm(x, norm_mix_pre[l])
        proj = jnp.einsum('bsd,de->bse', h, w_in[l])
        qa, ka, va, pu, qc, kc, vc, gl = jnp.split(proj, splits, axis=-1)
        ya = global_axial_attention(qa, ka, va, qk_norm_q[l], qk_norm_k[l])
        yb = pool_mixer(pu, w_pool[l], pool_scale[l])
        yc = neighbourhood_attention(qc, kc, vc, rpb[l])
        ys = jnp.stack([ya, yb, yc], axis=2)
        z = jnp.einsum('bsnc,ncd->bsnd', ys, w_branch[l])
        gates = jax.nn.sigmoid(gl + b_gate[l]).reshape(B, S, N_BRANCH, D_MODEL)
        merged = jnp.sum(gates * z, axis=2)
        mix = jnp.einsum('bsd,de->bse', merged, w_out[l])
        x = x + rmsnorm(mix, norm_mix_post[l])
        h = rmsnorm(x, norm_ffn_pre[l])
        f = conv_gated_mlp(h, w_up[l], conv_w[l], conv_b[l], w_down[l])
        x = x + rmsnorm(f, norm_ffn_post[l])
    return x
```

```python
import os
import numpy as np
from contextlib import ExitStack
import concourse.bass as bass
import concourse.mybir as mybir
from concourse.bass_utils import run_bass_kernel_spmd

F32 = mybir.dt.float32
BF16 = mybir.dt.bfloat16
AF = mybir.ActivationFunctionType
ALU = mybir.AluOpType

S = 2048
D = 1024
NSLOT = 140
P1_BASE, P2_BASE, P3_BASE = 0, 17, 74
NV = 222
V_GPM, V_GPF, V_BG, V_CW0, V_CW1, V_CW2, V_CB, V_PS, V_GQ, V_GK = 0, 8, 16, 40, 84, 128, 172, 216, 220, 221
NB = 8
NSTG = 2
LOOKAHEAD = 5
EPS = 1e-6


class Prog:
    ENG = ("pe", "act", "dve", "pool", "sp")

    def __init__(self, n_dma_sems=24, same_engine_sync=True):
        self.esem = {e: i for i, e in enumerate(self.ENG)}
        self.ND = n_dma_sems
        self.dbase = len(self.ENG)
        self.ecnt = {e: 0 for e in self.ENG}
        self.dcnt = [0] * self.ND
        self.dnext = 0
        self.seen = {e: {} for e in self.ENG}
        self.last_w = {}
        self.readers = {}
        self.ops = {e: [] for e in self.ENG}
        self.ses = same_engine_sync
        self.dry = False
        self.ps_last = {}
        self.nemit = 0
        self.maxops = int(os.environ.get("KDBG_MAXOPS", "1000000000"))
        self.log = []

    def emit(self, eng, fn, reads=(), writes=(), dma=False):
        if self.dry:
            return
        self.nemit += 1
        if self.nemit > self.maxops:
            return
        if os.environ.get("KDBG_LOG"):
            import inspect
            fr = inspect.stack()[1]
            self.log.append((self.nemit, eng, fr.lineno))
        waits = {}

        def need(tok):
            if tok is None:
                return
            s, v = tok
            if (not self.ses) and (not dma) and s == self.esem[eng]:
                return
            if self.seen[eng].get(s, 0) >= v:
                return
            if waits.get(s, 0) < v:
                waits[s] = v

        reads = list(reads)
        psr = [k for k in reads if k.startswith("ps") and k[2:].isdigit() and k not in writes]
        writes = list(writes) + psr
        reads = [k for k in reads if k not in writes]
        for k in reads:
            need(self.last_w.get(k))
        for k in writes:
            if k in psr and self.ps_last.get(k) == (eng, "r"):
                continue
            need(self.last_w.get(k))
            for t in self.readers.get(k, ()):
                need(t)
        for k in writes:
            if k.startswith("ps") and k[2:].isdigit():
                self.ps_last[k] = (eng, "r" if k in psr else "w")
        if dma:
            i = self.dnext
            self.dnext = (i + 1) % self.ND
            if self.dcnt[i] > 0:
                need((self.dbase + i, self.dcnt[i]))
            self.dcnt[i] += 16
            tok = (self.dbase + i, self.dcnt[i])
            amt = 16
        else:
            self.ecnt[eng] += 1
            tok = (self.esem[eng], self.ecnt[eng])
            amt = 1
        for s, v in waits.items():
            self.seen[eng][s] = v
        self.ops[eng].append((list(waits.items()), fn, tok[0], amt, False))
        for k in reads:
            self.readers.setdefault(k, []).append(tok)
        for k in writes:
            self.last_w[k] = tok
            self.readers[k] = []
        return tok

    def emit_ring_load(self, fn, b):
        if self.dry:
            return
        eng = "pool"
        key = f"ring{b}"
        sidx = self.dbase + self.ND + b
        waits = {}
        for tok in [self.last_w.get(key)] + list(self.readers.get(key, ())):
            if tok is None:
                continue
            s, v = tok
            if s == sidx:
                waits[s] = v
                continue
            if self.seen[eng].get(s, 0) >= v:
                continue
            if waits.get(s, 0) < v:
                waits[s] = v
        for s, v in waits.items():
            self.seen[eng][s] = v
        for e in self.ENG:
            self.seen[e].pop(sidx, None)
        self.ops[eng].append((list(waits.items()), fn, sidx, 16, True))
        self.last_w[key] = (sidx, 16)
        self.readers[key] = []

    def barrier(self):
        if self.dry:
            return
        allw = []
        for e in self.ENG:
            if self.ecnt[e] > 0:
                allw.append((self.esem[e], self.ecnt[e]))
        for i in range(self.ND):
            if self.dcnt[i] > 0:
                allw.append((self.dbase + i, self.dcnt[i]))
        for e in self.ENG:
            ws = []
            for s, v in allw:
                if self.seen[e].get(s, 0) >= v:
                    continue
                self.seen[e][s] = v
                ws.append((s, v))
            if ws:
                self.ops[e].append((ws, None, None, 0, False))
        self.ps_last = {}
        self.last_w = {k: v for k, v in self.last_w.items() if k.startswith("ring")}
        self.readers = {k: v for k, v in self.readers.items() if k.startswith("ring")}

    def finish(self, block, sems):
        self.barrier()

        def replay(ename):
            def run(eng):
                for waits, fn, s, amt, clr in self.ops[ename]:
                    for ws, wv in waits:
                        eng.wait_ge(sems[ws], wv)
                    if fn is not None:
                        if clr:
                            eng.sem_clear(sems[s])
                        ins = fn(eng)
                        ins.then_inc(sems[s], amt)
            return run

        block.tensor(replay("pe"))
        block.scalar(replay("act"))
        block.vector(replay("dve"))
        block.gpsimd(replay("pool"))
        block.sync(replay("sp"))


class Rot:
    def __init__(self, items):
        self.items = list(items)
        self.i = 0

    def next(self):
        v = self.items[self.i % len(self.items)]
        self.i += 1
        return v


def _e_slot(W, cols):
    sub = W[:, cols]
    return sub.reshape(8, 128, 128).transpose(1, 0, 2).reshape(128, 1024)


def build_wslots(w_in, w_pool, w_branch, w_out, w_up, w_down):
    sl = np.zeros((NSLOT, 128, 1024), np.float32)
    k = 0
    ar = np.arange
    sl[k] = _e_slot(w_in, ar(512, 640)); k += 1
    for c in range(4):
        sl[k] = _e_slot(w_in, ar(1792 + c * 128, 1792 + (c + 1) * 128)); k += 1
    for c in range(4):
        sl[k] = _e_slot(w_in, ar(768 + c * 128, 768 + (c + 1) * 128)); k += 1
    for kc in range(8):
        sl[k, :, 0:128] = w_in[kc * 128:(kc + 1) * 128, 640:768]
        sl[k, :, 128:640] = w_in[kc * 128:(kc + 1) * 128, 2304:2816]
        k += 1
    assert k == P2_BASE
    for c in range(4):
        cols = np.concatenate([ar(c * 64, c * 64 + 64), ar((4 + c) * 64, (4 + c) * 64 + 64)])
        sl[k] = _e_slot(w_in, cols); k += 1
    for c in range(4):
        sl[k] = _e_slot(w_in, ar(1280 + c * 128, 1280 + (c + 1) * 128)); k += 1
    for g in range(4):
        sl[k, :, g * 128:(g + 1) * 128] = w_pool[g]
    k += 1
    perm0 = np.concatenate([np.concatenate([ar(c * 64, c * 64 + 64), ar((4 + c) * 64, (4 + c) * 64 + 64)])
                            for c in range(4)])
    wb = [w_branch[0][perm0], w_branch[1], w_branch[2]]
    for dc in range(8):
        for n in range(3):
            sl[k] = _e_slot(w_in, ar(2816 + n * 1024 + dc * 128, 2816 + n * 1024 + (dc + 1) * 128)); k += 1
        for n in range(3):
            blk = wb[n][:, dc * 128:(dc + 1) * 128].reshape(4, 128, 128).transpose(1, 0, 2).reshape(128, 512)
            if n < 2:
                sl[k, :, n * 512:(n + 1) * 512] = blk
            else:
                sl[k + 1, :, 0:512] = blk
        k += 2
    for kc in range(8):
        sl[k] = w_out[kc * 128:(kc + 1) * 128, :]; k += 1
    assert k == P3_BASE
    for i in range(22):
        sl[k] = _e_slot(w_up, ar(i * 128, (i + 1) * 128)); k += 1
        sl[k] = _e_slot(w_up, ar(2816 + i * 128, 2816 + (i + 1) * 128)); k += 1
    for half in range(2):
        for j2 in range(11):
            for j in range(2):
                i = 2 * j2 + j
                sl[k, :, j * 512:(j + 1) * 512] = w_down[i * 128:(i + 1) * 128, half * 512:(half + 1) * 512]
            k += 1
    assert k == NSLOT
    return sl


def build_vecs(norm_mix_pre, norm_ffn_pre, b_gate, conv_w, conv_b, pool_scale, gq, gk):
    v = np.zeros((128, NV), np.float32)
    v[:, V_GPM:V_GPM + 8] = norm_mix_pre.reshape(8, 128).T
    v[:, V_GPF:V_GPF + 8] = norm_ffn_pre.reshape(8, 128).T
    v[:, V_BG:V_BG + 24] = b_gate.reshape(24, 128).T
    for j, base in enumerate((V_CW0, V_CW1, V_CW2)):
        v[:, base:base + 44] = conv_w[j].reshape(44, 128).T
    v[:, V_CB:V_CB + 44] = conv_b.reshape(44, 128).T
    v[:, V_PS:V_PS + 4] = pool_scale.reshape(4, 128).T
    v[:, V_GQ] = np.concatenate([gq, gq])
    v[:, V_GK] = np.concatenate([gk, gk])
    return v


def build_natab(rpb):
    kc = np.arange(64)[:, None]
    c = np.arange(64)[None, :]
    cs = np.clip(c - 8, 0, 48)
    valid = (kc >= cs) & (kc < cs + 16)
    dci = np.clip(kc - c + 15, 0, 30)
    g = rpb[:, :, dci]
    g = np.where(valid[None, None], g, np.float32(-200.0)).astype(np.float32)
    g = g.transpose(2, 0, 1, 3)
    g = np.concatenate([g, g], axis=0)
    return np.ascontiguousarray(g.reshape(128, 8 * 15 * 64))


def build_consts():
    cst = np.zeros((128, 128 * 3 + 64), np.float32)
    cst[:, 0:128] = np.eye(128, dtype=np.float32)
    p = np.arange(128)
    d = p % 64
    partner = np.where(d % 32 < 16, p + 16, p - 16)
    cst[partner, 128 + p] = 1.0
    cst[:, 256:384] = (p[:, None] // 64 == p[None, :] // 64).astype(np.float32)
    for g, w in enumerate((2, 4, 8, 16)):
        for t in range(8):
            cntl = min(t + w - w // 2, S) - max(t - w // 2, 0)
            cst[:, 384 + g * 16 + t] = w / cntl
            tr = S - 8 + t
            cntr = min(tr + w - w // 2, S) - max(tr - w // 2, 0)
            cst[:, 384 + g * 16 + 8 + t] = w / cntr
    t = np.arange(S)
    row = (t // 64).astype(np.float32)
    col = (t % 64).astype(np.float32)
    freqs = (np.float32(10000.0) ** (-(np.arange(0, 32, 2, dtype=np.float32)) / np.float32(32))).astype(np.float32)
    rope = np.zeros((2, 128, S), np.float32)
    for pp in range(128):
        dd = pp % 64
        pos = row if dd < 32 else col
        j = dd % 16
        ang = (pos * freqs[j]).astype(np.float32)
        sgn = -1.0 if (dd % 32) < 16 else 1.0
        rope[0, pp] = np.cos(ang)
        rope[1, pp] = sgn * np.sin(ang)
    return cst, rope


def build_nc(L=2, NSEQ=2, same_engine_sync=True, phases=(1, 2, 3)):
    nc = bass.Bass("TRN2", target_bir_lowering=False)
    x_in = nc.dram_tensor("x", [NSEQ, S, D], F32, kind="ExternalInput").ap()
    out = nc.dram_tensor("out", [NSEQ, S, D], F32, kind="ExternalOutput").ap()
    xsA = nc.dram_tensor("xsA", [NSEQ, S, D], F32, kind="Internal").ap()
    xsB = nc.dram_tensor("xsB", [NSEQ, S, D], F32, kind="Internal").ap()
    wsl = nc.dram_tensor("wslots", [L, NSLOT, 128, 1024], F32, kind="ExternalInput").ap()
    wbf = nc.dram_tensor("wbf", [L, NSLOT, 128, 1024], BF16, kind="Internal").ap()
    vecs_d = nc.dram_tensor("vecs", [L, 128, NV], F32, kind="ExternalInput").ap()
    gbp_d = nc.dram_tensor("gbp", [L, 2, D], F32, kind="ExternalInput").ap()
    natab_d = nc.dram_tensor("natab", [L, 128, 7680], F32, kind="ExternalInput").ap()
    cst_d = nc.dram_tensor("cst", [128, 448], F32, kind="ExternalInput").ap()
    rope_d = nc.dram_tensor("rope", [2, 128, S], F32, kind="ExternalInput").ap()

    with ExitStack() as es:
        def T(name, shape, dt):
            return es.enter_context(nc.sbuf_tensor("sb_" + name, shape, dt))

        cst = T("cst", [128, 448], F32)
        ident = T("ident", [128, 128], BF16)
        vecs = T("vecs", [128, NV], F32)
        bhalf = T("bhalf", [128, 24], F32)
        gq8 = T("gq8", [128, 1], F32)
        gbp = T("gbp", [128, 2, D], F32)
        M2 = T("M2", [128, 8, 15, 64], BF16)
        ropeC = T("ropeC", [128, 512], F32)
        ropeS = T("ropeS", [128, 512], F32)
        KA_T = T("KA_T", [128, S], BF16)
        VA1 = T("VA1", [128, 16 * 192], BF16)
        KC_T = T("KC_T", [128, 4, S], BF16)
        VC1 = T("VC1", [128, 16 * 768], BF16)
        diff_T = T("diff_T", [128, 4, S], BF16)
        xtb = [T(f"xt{i}", [128, D], F32) for i in range(2)]
        xnb = [T(f"xn{i}", [128, D], BF16) for i in range(2)]
        hT = T("hT", [128, 8, 514], BF16)
        fb = [T(f"fb{i}", [128, 512], F32) for i in range(8)]
        junk = T("junk", [128, D], BF16)
        tmpn = T("tmpn", [128, D], F32)
        stat = T("stat", [128, 24], F32)
        halo_sb = T("halo_sb", [128, 16], F32)
        xh = T("xh", [2, D], F32)
        xnh = T("xnh", [2, D], BF16)
        ring = T("ring", [128, NB, 1024], BF16)
        stg2 = T("stg2", [128, NSTG, 1024], F32)
        stg = [stg2[:, i, :] for i in range(NSTG)]
        hT2 = stg2[:].rearrange("p a d -> p (a d)").bitcast(BF16).rearrange("p (c t) -> p c t", c=8)
        HT2_KEYS = ["stg0", "stg1", "hT2"]
        ARENA = 38912
        arena = T("arena", [128, ARENA // 2], BF16)

        def aview(off_b, nbytes, dt):
            a = arena[:, off_b // 2:(off_b + nbytes) // 2]
            return a if dt == BF16 else a.bitcast(dt)

        u_f = aview(0, 4 * 2064 * 4, F32).rearrange("p (g t) -> p g t", g=4)
        o = 0
        qAz = aview(o, 8192, BF16).rearrange("p (h t) -> p h t", h=8); o += 8192
        qC_T = aview(o, 4096, BF16).rearrange("p (c t) -> p c t", c=4); o += 4096
        y_T = [aview(o + n * 4096, 4096, BF16).rearrange("p (c t) -> p c t", c=4) for n in range(3)]; o += 12288
        Pn = [aview(o + i * 2048, 2048, BF16).rearrange("p (h q) -> p h q", h=8) for i in range(5)]
        merged_T = aview(o, 8192, BF16).rearrange("p (c t) -> p c t", c=8)
        o += 10240
        pTb = [aview(o + i * 1024, 1024, BF16) for i in range(4)]
        Pn2 = [aview(i * 2048, 2048, BF16).rearrange("p (h q) -> p h q", h=8) for i in range(4)] + \
              [aview(o, 2048, BF16).rearrange("p (h q) -> p h q", h=8)]
        Pn2_keys = [[f"PnB{i}"] for i in range(5)]
        o += 4096
        assert o <= ARENA
        act_T = aview(0, 22528, BF16).rearrange("p (i t) -> p i t", i=22)
        f_sb = [aview(22528 + i * 4096, 4096, F32) for i in range(4)]
        assert 22528 + 16384 <= ARENA

        ps = [es.enter_context(nc.psum_tensor(f"ps{i}", [128, 512], F32)) for i in range(8)]
        psT = ps[7][:].bitcast(BF16)
        NSEM = 5 + 24
        sems = [es.enter_context(nc.semaphore(f"s{i}")) for i in range(NSEM)]
        block = es.enter_context(nc.Block())
        P = Prog(n_dma_sems=24, same_engine_sync=same_engine_sync)

        identf = cst[:, 0:128]
        Rm_f = cst[:, 128:256]
        BD_f = cst[:, 256:384]
        edge = cst[:, 384:448].rearrange("p (g t) -> p g t", g=4)

        class WS:
            def __init__(self):
                self.reqs = []
                self.ptr = 0
                self.issued = 0
                self.converted = set()
                self.nconv = 0
                self.pending_store = {}

            def next(self, l, idx):
                if P.dry:
                    self.reqs.append((l, idx))
                    return None, None
                assert self.reqs[self.ptr] == (l, idx), (self.reqs[self.ptr], l, idx)
                while self.issued < min(len(self.reqs), self.ptr + 1 + LOOKAHEAD):
                    ll, ii = self.reqs[self.issued]
                    b = self.issued % NB
                    if (ll, ii) not in self.converted:
                        self.converted.add((ll, ii))
                        sg = self.nconv % NSTG
                        self.nconv += 1
                        P.emit("sp", (lambda e, ll=ll, ii=ii, sg=sg: e.dma_start(out=stg[sg], in_=wsl[ll, ii])),
                               writes=[f"stg{sg}"], dma=True)
                        if self.nconv % 2:
                            P.emit("act", (lambda e, b=b, sg=sg: e.activation(out=ring[:, b, :], in_=stg[sg], func=AF.Copy)),
                                   reads=[f"stg{sg}"], writes=[f"ring{b}"])
                        else:
                            P.emit("dve", (lambda e, b=b, sg=sg: e.tensor_copy(out=ring[:, b, :], in_=stg[sg])),
                                   reads=[f"stg{sg}"], writes=[f"ring{b}"])
                        self.pending_store[self.issued] = (ll, ii)
                    else:
                        P.emit("sp", (lambda e, ll=ll, ii=ii, b=b: e.dma_start(out=ring[:, b, :], in_=wbf[ll, ii])),
                               reads=[f"wbf{ll}_{ii}"], writes=[f"ring{b}"], dma=True)
                    self.issued += 1
                b = self.ptr % NB
                if self.ptr in self.pending_store:
                    ll, ii = self.pending_store.pop(self.ptr)
                    P.emit("sp", (lambda e, ll=ll, ii=ii, b=b: e.dma_start(out=wbf[ll, ii], in_=ring[:, b, :])),
                           reads=[f"ring{b}"], writes=[f"wbf{ll}_{ii}"], dma=True)
                self.ptr += 1
                return ring[:, b, :], f"ring{b}"

        ws = WS()

        def V1ap(t, nh, h, is_c):
            if is_c:
                c0 = t * 768 + (h // 2) * 192 + (h % 2) * 64
                return VC1[:, c0:c0 + 128]
            c0 = t * 192 + h * 64
            return VA1[:, c0:c0 + 128]

        def load_consts():
            P.emit("sp", lambda e: e.dma_start(out=cst[:], in_=cst_d[:, :]), writes=["cst"], dma=True)
            P.emit("dve", lambda e: e.tensor_copy(out=ident[:], in_=cst[:, 0:128]), reads=["cst"], writes=["ident"])
            P.emit("pool", lambda e: e.memset(VA1[:].rearrange("p (t a b) -> p t a b", a=3, b=64)[:, :, 1, :], 1.0), writes=["VA1"])
            P.emit("pool", lambda e: e.memset(VC1[:].rearrange("p (t a b) -> p t a b", a=3, b=64)[:, :, 1, :], 1.0), writes=["VC1"])
            P.barrier()

        def precast_weights(l):
            nst = 9
            stgs = [aview(i * 4096, 4096, F32) for i in range(nst)]
            engs = ("act", "dve", "act", "dve", "act", "dve", "act", "pool")
            for idx in range(NSLOT):
                sg = idx % nst
                b = idx % NB
                P.emit("sp", (lambda e, idx=idx, sg=sg: e.dma_start(out=stgs[sg][:], in_=wsl[l, idx])),
                       writes=[f"pstg{sg}"], dma=True)
                en = engs[idx % len(engs)]
                if en == "act":
                    P.emit("act", (lambda e, b=b, sg=sg: e.activation(out=ring[:, b, :], in_=stgs[sg][:], func=AF.Copy)),
                           reads=[f"pstg{sg}"], writes=[f"ring{b}"])
                else:
                    P.emit(en, (lambda e, b=b, sg=sg: e.tensor_copy(out=ring[:, b, :], in_=stgs[sg][:])),
                           reads=[f"pstg{sg}"], writes=[f"ring{b}"])
                P.emit("sp", (lambda e, idx=idx, b=b: e.dma_start(out=wbf[l, idx], in_=ring[:, b, :])),
                       reads=[f"ring{b}"], writes=[f"wbf{idx}"], dma=True)

        def load_layer_consts(l):
            P.barrier()
            P.emit("sp", lambda e: e.dma_start(out=vecs[:], in_=vecs_d[l]), writes=["vecs"], dma=True)
            P.emit("sp", lambda e: e.dma_start(out=gbp[:].rearrange("p a d -> p (a d)"),
                                                in_=gbp_d[l].rearrange("a d -> (a d)").partition_broadcast(128)),
                   writes=["gbp"], dma=True)
            P.emit("dve", lambda e: e.tensor_scalar(out=bhalf[:], in0=vecs[:, V_BG:V_BG + 24], scalar1=0.5, scalar2=None,
                                                    op0=ALU.mult), reads=["vecs"], writes=["bhalf"])
            P.emit("dve", lambda e: e.tensor_scalar(out=gq8[:], in0=vecs[:, V_GQ:V_GQ + 1], scalar1=0.125, scalar2=None,
                                                    op0=ALU.mult), reads=["vecs"], writes=["gq8"])
            M2f = M2[:].rearrange("p h r c -> p (h r c)")
            for i in range(8):
                P.emit("sp", lambda e, i=i: e.dma_start(out=tmpn[:, 0:960], in_=natab_d[l][:, i * 960:(i + 1) * 960]),
                       writes=["tmpn"], dma=True)
                P.emit("act", lambda e, i=i: e.activation(out=M2f[:, i * 960:(i + 1) * 960], in_=tmpn[:, 0:960], func=AF.Exp),
                       reads=["tmpn"], writes=["M2"])
            P.barrier()

        def xkey(buf, s, tile):
            return f"x{buf}_{s}_{tile}"

        def rms_rstd(ss_ap, scale, rstd_ap, rkeys, wkey):
            P.emit("act", lambda e: e.activation(out=rstd_ap, in_=ss_ap, func=AF.Ln, scale=scale, bias=EPS),
                   reads=rkeys, writes=[wkey])
            P.emit("act", lambda e: e.activation(out=rstd_ap, in_=rstd_ap, func=AF.Exp, scale=-0.5),
                   reads=[wkey], writes=[wkey])

        xn4 = [(xnb[0][:], "xn0"), (xnb[1][:], "xn1"),
               (tmpn[:].bitcast(BF16)[:, 0:1024], "tmpn"), (tmpn[:].bitcast(BF16)[:, 1024:2048], "tmpn")]

        def norm_A(src, sname, s, g, tt, xsel=None):
            tile = g * 4 + tt
            xt = xtb[tt % 2]
            xn, kn = (xnb[tt % 2][:], f"xn{tt % 2}") if xsel is None else (xn4[xsel][0], xn4[xsel][1])
            kx = f"xt{tt % 2}"
            ss = stat[:, tt % 2:tt % 2 + 1]
            rs = stat[:, 2 + tt % 2:3 + tt % 2]
            P.emit("sp", lambda e: e.dma_start(out=xt[:], in_=src[s, tile * 128:(tile + 1) * 128, :]),
                   reads=[xkey(sname, s, tile)], writes=[kx], dma=True)
            P.emit("act", lambda e: e.activation(out=junk[:], in_=xt[:], func=AF.Square, accum_out=ss),
                   reads=[kx], writes=["junk", f"ss{tt % 2}"])
            rms_rstd(ss, 1.0 / D, rs, [f"ss{tt % 2}"], f"rs{tt % 2}")
            P.emit("dve", lambda e: e.tensor_scalar(out=xn, in0=xt[:], scalar1=rs, scalar2=None, op0=ALU.mult),
                   reads=[kx, f"rs{tt % 2}"], writes=[kn])

        def norm_B(tt, gcol, col0, hbuf=None, hkeys=("hT",), xsel=None):
            xn, kn = (xnb[tt % 2][:], f"xn{tt % 2}") if xsel is None else (xn4[xsel][0], xn4[xsel][1])
            if hbuf is None:
                hbuf = hT

            def tr(e):
                for c in range(8):
                    ins = e.transpose(out=psT[:, c * 128:(c + 1) * 128], in_=xn[:, c * 128:(c + 1) * 128], identity=ident[:])
                return ins
            P.emit("pe", tr, reads=[kn], writes=["ps7"])
            P.emit("dve", lambda e: e.tensor_tensor(
                out=hbuf[:, :, col0 + tt * 128:col0 + (tt + 1) * 128], in0=psT.rearrange("p (c t) -> p c t", c=8),
                in1=vecs[:, gcol:gcol + 8].unsqueeze(2).to_broadcast([128, 8, 128]), op=ALU.mult),
                reads=["ps7"], writes=list(hkeys))

        def norm_group(src, sname, s, g, gcol, col0, hbuf=None, hkeys=("hT",)):
            for tt in range(4):
                norm_A(src, sname, s, g, tt)
                norm_B(tt, gcol, col0, hbuf, hkeys)

        def proj_e(l, idx, psb, col0=0, n=512, hbuf=None, hkeys=("hT",)):
            rb, rk = ws.next(l, idx)
            if P.dry:
                return
            w = rb.rearrange("p (kc e) -> p kc e", kc=8)
            if hbuf is None:
                hbuf = hT

            def mm(e):
                for kc in range(8):
                    ins = e.matmul(ps[psb][:, 0:n], lhsT=w[:, kc, :], rhs=hbuf[:, kc, col0:col0 + n],
                                   start=(kc == 0), stop=(kc == 7))
                return ins
            P.emit("pe", mm, reads=[rk] + list(hkeys), writes=[f"ps{psb}"])

        fbrot = Rot(range(8))

        def qk_pipeline(psb, gvec, out_ap, okey):
            a, b, c = fbrot.next(), fbrot.next(), fbrot.next()
            A, B, C = fb[a], fb[b], fb[c]
            ka, kb, kc_ = f"fb{a}", f"fb{b}", f"fb{c}"
            P.emit("act", lambda e: e.activation(out=A[:], in_=ps[psb][:], func=AF.Square), reads=[f"ps{psb}"], writes=[ka])
            P.emit("pe", lambda e: e.matmul(ps[4][:], lhsT=BD_f, rhs=A[:], start=True, stop=True), reads=[ka], writes=["ps4"])
            P.emit("act", lambda e: e.activation(out=B[:], in_=ps[4][:], func=AF.Ln, scale=1.0 / 64, bias=EPS),
                   reads=["ps4"], writes=[kb])
            P.emit("act", lambda e: e.activation(out=B[:], in_=B[:], func=AF.Exp, scale=-0.5), reads=[kb], writes=[kb])
            P.emit("dve", lambda e: e.scalar_tensor_tensor(out=C[:], in0=ps[psb][:], scalar=gvec, in1=B[:], op0=ALU.mult,
                                                           op1=ALU.mult), reads=[f"ps{psb}", kb], writes=[kc_])
            P.emit("pe", lambda e: e.matmul(ps[5][:], lhsT=Rm_f, rhs=C[:], start=True, stop=True), reads=[kc_], writes=["ps5"])
            P.emit("dve", lambda e: e.tensor_tensor(out=A[:], in0=C[:], in1=ropeC[:], op=ALU.mult), reads=[kc_, "rope"], writes=[ka])
            P.emit("dve", lambda e: e.tensor_tensor(out=B[:], in0=ps[5][:], in1=ropeS[:], op=ALU.mult), reads=["ps5", "rope"],
                   writes=[kb])
            if isinstance(out_ap, list):
                for (psl, oap) in out_ap:
                    P.emit("pool", lambda e, psl=psl, oap=oap: e.tensor_tensor(out=oap, in0=A[psl, :], in1=B[psl, :], op=ALU.add),
                           reads=[ka, kb], writes=[okey])
            else:
                P.emit("pool", lambda e: e.tensor_tensor(out=out_ap, in0=A[:], in1=B[:], op=ALU.add), reads=[ka, kb], writes=[okey])

        def load_rope(g):
            P.emit("sp", lambda e: e.dma_start(out=ropeC[:], in_=rope_d[0][:, g * 512:(g + 1) * 512]), writes=["rope"], dma=True)
            P.emit("sp", lambda e: e.dma_start(out=ropeS[:], in_=rope_d[1][:, g * 512:(g + 1) * 512]), writes=["rope"], dma=True)

        def phase1(l, s, src, sname):
            P.emit("pool", lambda e: e.memset(u_f[:, :, 0:8], 0.0), writes=["u_fpad"])
            P.emit("pool", lambda e: e.memset(u_f[:, :, 2056:2064], 0.0), writes=["u_fpad"])

            def p1_group(g):
                if g == 0:
                    load_rope(g)
                    norm_group(src, sname, s, g, V_GPM, 0)
                tsl = slice(g * 512, (g + 1) * 512)
                proj_e(l, P1_BASE + 0, 0)
                qk_pipeline(0, vecs[:, V_GK:V_GK + 1], KA_T[:, tsl], "KA_T")
                for c in range(4):
                    pb = 1 + c % 3
                    proj_e(l, P1_BASE + 1 + c, pb)
                    P.emit("act", lambda e, c=c, pb=pb: e.activation(out=KC_T[:, c, tsl], in_=ps[pb][:], func=AF.Copy),
                           reads=[f"ps{pb}"], writes=["KC_T"])
                for c in range(4):
                    pb = 1 + (c + 1) % 3
                    proj_e(l, P1_BASE + 5 + c, pb)
                    P.emit("dve", lambda e, c=c, pb=pb: e.tensor_copy(out=u_f[:, c, 8 + g * 512:8 + (g + 1) * 512], in_=ps[pb][:]),
                           reads=[f"ps{pb}"], writes=[f"u_f{g}"])
                if g < 3 and not P.dry:
                    load_rope(g + 1)
                    for tt in range(4):
                        norm_A(src, sname, s, g + 1, tt, tt)

                def v_kc(kc):
                    rb, rk = ws.next(l, P1_BASE + 9 + kc)
                    if P.dry:
                        return

                    def mmv(e):
                        for tt in range(4):
                            e.matmul(ps[4 + tt][:, 0:128], lhsT=hT[:, kc, tt * 128:(tt + 1) * 128], rhs=rb[:, 0:128],
                                     start=(kc == 0), stop=(kc == 7))
                            ins = e.matmul(ps[tt][:], lhsT=hT[:, kc, tt * 128:(tt + 1) * 128], rhs=rb[:, 128:640],
                                           start=(kc == 0), stop=(kc == 7))
                        return ins
                    P.emit("pe", mmv, reads=["hT", rk], writes=[f"ps{i}" for i in range(8)])
                for kc in range(8):
                    v_kc(kc)
                if P.dry:
                    return
                for tt in range(4):
                    tile = g * 4 + tt
                    P.emit("act", lambda e, tile=tile, tt=tt: e.activation(
                        out=VA1[:, tile * 192:(tile + 1) * 192].rearrange("p (a b) -> p a b", b=64)[:, 0:3:2, :],
                        in_=ps[4 + tt][:, 0:128].rearrange("p (a b) -> p a b", b=64), func=AF.Copy),
                        reads=[f"ps{4 + tt}"], writes=["VA1"])
                    P.emit("dve", lambda e, tile=tile, tt=tt: e.tensor_copy(
                        out=VC1[:, tile * 768:(tile + 1) * 768].rearrange("p (r a b) -> p r a b", a=3, b=64)[:, :, 0:3:2, :],
                        in_=ps[tt][:].rearrange("p (r a b) -> p r a b", a=2, b=64)), reads=[f"ps{tt}"], writes=["VC1"])
                if g < 3:
                    for tt in range(4):
                        norm_B(tt, V_GPM, 0, None, ("hT",), tt)
            for g in range(4):
                p1_group(g)
            if not P.dry:
                bufs = [(tmpn, "tmpn"), (xtb[0], "xt0")]
                rk_all = ["u_f0", "u_f1", "u_f2", "u_f3", "u_fpad"]
                for gi, w in enumerate((2, 4, 8, 16)):
                    for (ca, cb_) in ((0, 1008), (1008, 2016), (2016, 2048)):
                        pool_chunk(gi, w, ca, cb_, bufs, "dve", rk_all, "diff_T")

        def pool_chunk(gi, w, ca, cb_, bufs, eng, rkeys, dkey):
            W = cb_ - ca + 16
            assert W <= 1024
            U = u_f[:, gi, ca:ca + W]
            (cur, ck), (oth, ok) = bufs
            P.emit(eng, lambda e, cur=cur: e.tensor_tensor(out=cur[:, 1:W], in0=U[:, 0:W - 1], in1=U[:, 1:W], op=ALU.add),
                   reads=rkeys, writes=[ck])
            lo, hi, sh = 1, W, 1
            for _ in range(gi):
                nlo, nhi = lo + sh, hi - sh
                P.emit(eng, lambda e, cur=cur, oth=oth, nlo=nlo, nhi=nhi, sh=sh: e.tensor_tensor(
                    out=oth[:, nlo:nhi], in0=cur[:, nlo - sh:nhi - sh], in1=cur[:, nlo + sh:nhi + sh], op=ALU.add),
                    reads=[ck], writes=[ok])
                cur, oth, ck, ok = oth, cur, ok, ck
                lo, hi = nlo, nhi
                sh *= 2
            assert lo <= 8 and hi >= W - 8
            if ca == 0:
                P.emit(eng, lambda e, cur=cur: e.tensor_tensor(out=cur[:, 8:16], in0=cur[:, 8:16], in1=edge[:, gi, 0:8],
                                                               op=ALU.mult), reads=[ck], writes=[ck])
            if cb_ == S:
                P.emit(eng, lambda e, cur=cur: e.tensor_tensor(out=cur[:, W - 16:W - 8], in0=cur[:, W - 16:W - 8],
                                                               in1=edge[:, gi, 8:16], op=ALU.mult), reads=[ck], writes=[ck])
            P.emit("dve", lambda e, cur=cur: e.scalar_tensor_tensor(
                out=diff_T[:, gi, ca:cb_], in0=cur[:, 8:W - 8], scalar=1.0 / w, in1=U[:, 8:W - 8], op0=ALU.mult, op1=ALU.subtract),
                reads=[ck] + rkeys, writes=[dkey])

        def phase2(l, s, src, sname, dst, dname):
            for g in range(4):
                phase2_group(l, s, src, sname, dst, dname, g)

        def p2_prep_q():
            P.emit("pool", lambda e: e.memset(qAz[64:128, 0:4, :], 0.0), writes=["qA_T", "pT0", "pT1"] + [f"PnB{i}" for i in range(5)])
            P.emit("pool", lambda e: e.memset(qAz[0:64, 4:8, :], 0.0), writes=["qA_T"])

        def phase2_group(l, s, src, sname, dst, dname, g):
            hb, hk = hT, ("hT",)
            if True:
                if g == 0:
                    load_rope(g)
                    p2_prep_q()
                    norm_group(src, sname, s, g, V_GPM, 0, hb, hk)
                qreq = [P2_BASE + c for c in range(4)] + [P2_BASE + 4 + c for c in range(4)]
                for c in range(4):
                    pb = (2 * c) % 4
                    proj_e(l, P2_BASE + c, pb, hbuf=hb, hkeys=hk)
                    qk_pipeline(pb, gq8[:, 0:1], [(slice(0, 64), qAz[0:64, c, :]), (slice(64, 128), qAz[64:128, 4 + c, :])], "qA_T")
                    pb2 = (2 * c + 1) % 4
                    proj_e(l, P2_BASE + 4 + c, pb2, hbuf=hb, hkeys=hk)
                    P.emit("act", lambda e, c=c, pb2=pb2: e.activation(out=qC_T[:, c, :], in_=ps[pb2][:], func=AF.Copy, scale=0.125),
                           reads=[f"ps{pb2}"], writes=["qC_T"])
                items = [(h, st) for h in range(8) for st in range(16)]
                srot = Rot((0, 1, 2))
                prot = Rot(range(4))
                pend = []

                def pv(h, st, pi):
                    ob = 3 + h % 2
                    P.emit("pe", lambda e: e.matmul(ps[ob][:], lhsT=V1ap(st, 2, h // 4, False), rhs=pTb[pi][:],
                                                    start=(st == 0), stop=(st == 15)),
                           reads=[f"pT{pi}", "VA1"], writes=[f"ps{ob}"])
                    if st == 15:
                        r = fbrot.next()
                        c, half = h % 4, h // 4
                        ns = slice(half * 64, half * 64 + 64)
                        ds = slice((1 - half) * 64, (1 - half) * 64 + 64)
                        P.emit("dve", lambda e: e.reciprocal(out=fb[r][ns, :], in_=ps[ob][ds, :]), reads=[f"ps{ob}"],
                               writes=[f"fb{r}"])
                        P.emit("dve", lambda e: e.tensor_tensor(out=y_T[0][ns, c, :], in0=ps[ob][ns, :],
                                                                in1=fb[r][ns, :], op=ALU.mult),
                               reads=[f"ps{ob}", f"fb{r}"], writes=["y0"])
                def run_attention(hook):
                    for (h, st) in items:
                        sb = srot.next()
                        pi = prot.next()
                        P.emit("pe", lambda e, h=h, st=st, sb=sb: e.matmul(
                            ps[sb][:], lhsT=KA_T[:, st * 128:(st + 1) * 128], rhs=qAz[:, h, :], start=True, stop=True),
                            reads=["KA_T", "qA_T"], writes=[f"ps{sb}"])
                        P.emit("act", lambda e, sb=sb, pi=pi: e.activation(out=pTb[pi][:], in_=ps[sb][:], func=AF.Exp),
                               reads=[f"ps{sb}"], writes=[f"pT{pi}"])
                        pend.append((h, st, pi))
                        if len(pend) > 2:
                            pv(*pend.pop(0))
                        if st == 15:
                            hook(h)
                    while pend:
                        pv(*pend.pop(0))
                def na_tile(jl):
                    J = g * 4 + jl
                    r0, r1 = 2 * J, 2 * J + 1
                    rs0, rs1 = min(max(r0 - 4, 0), 24), min(max(r1 - 4, 0), 24)
                    tiles = list(range(rs0 // 2, (rs1 + 7) // 2 + 1))
                    pset, pkeys = Pn, [[f"Pn{i}"] for i in range(5)]
                    qsl = slice(jl * 128, (jl + 1) * 128)
                    def na_ktile(i, t):
                        pn = pset[i]
                        pk = pkeys[i][-1]
                        pks = pkeys[i]
                        for par in range(2):
                            sb = srot.next()

                            def mmq(e, t=t, par=par, sb=sb):
                                hs = slice(par * 64, par * 64 + 64)
                                for a4 in range(4):
                                    ins = e.matmul(ps[sb][:, a4 * 128:(a4 + 1) * 128], lhsT=KC_T[hs, a4, t * 128:(t + 1) * 128],
                                                   rhs=qC_T[hs, a4, qsl], start=True, stop=True)
                                return ins
                            P.emit("pe", mmq, reads=["KC_T", "qC_T"], writes=[f"ps{sb}"])
                            P.emit("act", lambda e, pn=pn, par=par, sb=sb: e.activation(
                                out=pn[:, par:8:2, :], in_=ps[sb][:].rearrange("p (h q) -> p h q", h=4), func=AF.Exp),
                                reads=[f"ps{sb}"], writes=pks)
                        for kl in range(2):
                            for ql in range(2):
                                kr, r = 2 * t + kl, 2 * J + ql
                                rs = min(max(r - 4, 0), 24)
                                ksl = slice(kl * 64, kl * 64 + 64)
                                qs2 = slice(ql * 64, ql * 64 + 64)
                                if rs <= kr < rs + 8:
                                    dri = kr - r + 7
                                    P.emit("dve", lambda e, pn=pn, ksl=ksl, qs2=qs2, dri=dri: e.tensor_tensor(
                                        out=pn[ksl, :, qs2], in0=pn[ksl, :, qs2], in1=M2[ksl, :, dri, :], op=ALU.mult),
                                        reads=pks, writes=pks)
                                else:
                                    P.emit("pool", lambda e, pn=pn, ksl=ksl, qs2=qs2: e.memset(pn[ksl, :, qs2], 0.0),
                                           reads=pks, writes=pks)
                    for i, t in enumerate(tiles):
                        na_ktile(i, t)
                    return (tiles, pset, pkeys, qsl)

                def na_stage2(ctx):
                    tiles, pset, pkeys, qsl = ctx
                    for hg in range(2):
                        ob = 5 + hg

                        def mmpv(e, hg=hg, ob=ob):
                            for hh in range(4):
                                h = hg * 4 + hh
                                for i, t in enumerate(tiles):
                                    ins = e.matmul(ps[ob][:, hh * 128:(hh + 1) * 128], lhsT=V1ap(t, 8, h, True),
                                                   rhs=pset[i][:, h, :], start=(i == 0), stop=(i == len(tiles) - 1))
                            return ins
                        P.emit("pe", mmpv, reads=[k for i in range(len(tiles)) for k in pkeys[i]] + ["VC1"], writes=[f"ps{ob}"])

                def na_norm(ctx):
                    tiles, pset, pkeys, qsl = ctx
                    for hg in range(2):
                        ob = 5 + hg
                        r = fbrot.next()
                        for par in range(2):
                            ns = slice(par * 64, par * 64 + 64)
                            ds = slice((1 - par) * 64, (1 - par) * 64 + 64)
                            P.emit("act", lambda e, r=r, ob=ob, par=par, ns=ns, ds=ds: e.activation(
                                out=fb[r][ns, :].rearrange("p (a b q) -> p a b q", a=2, b=2)[:, :, par, :],
                                in_=ps[ob][ds, :].rearrange("p (a b q) -> p a b q", a=2, b=2)[:, :, par, :], func=AF.Ln),
                                reads=[f"ps{ob}"], writes=[f"fb{r}"])
                            P.emit("act", lambda e, r=r, par=par, ns=ns: e.activation(
                                out=fb[r][ns, :].rearrange("p (a b q) -> p a b q", a=2, b=2)[:, :, par, :],
                                in_=fb[r][ns, :].rearrange("p (a b q) -> p a b q", a=2, b=2)[:, :, par, :], func=AF.Exp, scale=-1.0),
                                reads=[f"fb{r}"], writes=[f"fb{r}"])
                            P.emit("dve", lambda e, r=r, ob=ob, par=par, hg=hg, ns=ns: e.tensor_tensor(
                                out=y_T[2][ns, hg * 2:(hg + 1) * 2, qsl],
                                in0=ps[ob][ns, :].rearrange("p (a b q) -> p a b q", a=2, b=2)[:, :, par, :],
                                in1=fb[r][ns, :].rearrange("p (a b q) -> p a b q", a=2, b=2)[:, :, par, :], op=ALU.mult),
                                reads=[f"ps{ob}", f"fb{r}"], writes=["y2"])
                ctxs = {}

                def na_hook(h):
                    if h % 2 == 0:
                        if h >= 2:
                            na_norm(ctxs.pop(h // 2 - 1))
                        ctxs[h // 2] = na_tile(h // 2)
                    else:
                        na_stage2(ctxs[h // 2])
                run_attention(na_hook)
                na_norm(ctxs.pop(3))
                rb, rk = ws.next(l, P2_BASE + 8)
                if not P.dry:
                    for gi in range(4):
                        pb = gi % 3
                        P.emit("pe", lambda e, gi=gi, pb=pb: e.matmul(ps[pb][:], lhsT=rb[:, gi * 128:(gi + 1) * 128],
                                                                      rhs=diff_T[:, gi, g * 512:(g + 1) * 512], start=True, stop=True),
                               reads=[rk, "diff_T"], writes=[f"ps{pb}"])
                        P.emit("dve", lambda e, gi=gi, pb=pb: e.tensor_scalar(out=y_T[1][:, gi, :], in0=ps[pb][:],
                                                                              scalar1=vecs[:, V_PS + gi:V_PS + gi + 1], scalar2=None,
                                                                              op0=ALU.mult), reads=[f"ps{pb}"], writes=["y1"])
                grot = Rot((0, 1, 2))
                zrot = Rot((3, 4, 5))
                def merge_dc(dc):
                    base = P2_BASE + 9 + dc * 5
                    acc_i = fbrot.next()
                    brs = {}
                    for n in range(3):
                        merge_n(dc, n, base, acc_i, brs)

                def merge_n(dc, n, base, acc_i, brs):
                    if True:
                        gb_ = grot.next()
                        zb = zrot.next()
                        proj_e(l, base + n, gb_, hbuf=hb, hkeys=hk)
                        tg = fbrot.next()
                        P.emit("act", lambda e, gb_=gb_, tg=tg, n=n, dc=dc: e.activation(
                            out=fb[tg][:], in_=ps[gb_][:], func=AF.Tanh, scale=0.5, bias=bhalf[:, n * 8 + dc:n * 8 + dc + 1]),
                            reads=[f"ps{gb_}"], writes=[f"fb{tg}"])
                        if n == 0:
                            brs["A"] = ws.next(l, base + 3)
                            brs["B"] = ws.next(l, base + 4)
                        if P.dry:
                            return
                        bsl, bk = (brs["A"] if n < 2 else brs["B"])
                        joff = (n % 2) * 512

                        def mmz(e, n=n, zb=zb, bsl=bsl, joff=joff):
                            for kc in range(4):
                                ins = e.matmul(ps[zb][:], lhsT=bsl[:, joff + kc * 128:joff + (kc + 1) * 128], rhs=y_T[n][:, kc, :],
                                               start=(kc == 0), stop=(kc == 3))
                            return ins
                        P.emit("pe", mmz, reads=[bk, f"y{n}"], writes=[f"ps{zb}"])
                        if n == 0:
                            P.emit("dve", lambda e, tg=tg, zb=zb, acc_i=acc_i: e.scalar_tensor_tensor(
                                out=fb[acc_i][:], in0=fb[tg][:], scalar=1.0, in1=ps[zb][:], op0=ALU.add, op1=ALU.mult),
                                reads=[f"fb{tg}", f"ps{zb}"], writes=[f"fb{acc_i}"])
                        else:
                            P.emit("dve", lambda e, tg=tg, zb=zb: e.scalar_tensor_tensor(
                                out=fb[tg][:], in0=fb[tg][:], scalar=1.0, in1=ps[zb][:], op0=ALU.add, op1=ALU.mult),
                                reads=[f"fb{tg}", f"ps{zb}"], writes=[f"fb{tg}"])
                            if n == 1:
                                P.emit("pool", lambda e, tg=tg, acc_i=acc_i: e.tensor_tensor(
                                    out=fb[acc_i][:], in0=fb[acc_i][:], in1=fb[tg][:], op=ALU.add),
                                    reads=[f"fb{tg}", f"fb{acc_i}"], writes=[f"fb{acc_i}"])
                            else:
                                P.emit("pool", lambda e, tg=tg, acc_i=acc_i, dc=dc: e.tensor_tensor(
                                    out=merged_T[:, dc, :], in0=fb[acc_i][:], in1=fb[tg][:], op=ALU.add),
                                    reads=[f"fb{tg}", f"fb{acc_i}"], writes=["Pn0", "Pn1", "Pn2", "Pn3", "merged"])
                for dc in range(8):
                    merge_dc(dc)
                    if g < 3 and not P.dry:
                        if dc == 0:
                            load_rope(g + 1)
                            p2_prep_q()
                            norm_A(src, sname, s, g + 1, 0, 0)
                            norm_A(src, sname, s, g + 1, 1, 1)
                        elif dc == 2:
                            norm_A(src, sname, s, g + 1, 2, 2)
                        elif dc == 4:
                            norm_A(src, sname, s, g + 1, 3, 3)
                def wout_kc(kc):
                    rb, rk = ws.next(l, P2_BASE + 49 + kc)
                    if P.dry:
                        return

                    def mmo(e):
                        for tt in range(4):
                            for half in range(2):
                                ins = e.matmul(ps[tt * 2 + half][:], lhsT=merged_T[:, kc, tt * 128:(tt + 1) * 128],
                                               rhs=rb[:, half * 512:(half + 1) * 512], start=(kc == 0), stop=(kc == 7))
                        return ins
                    P.emit("pe", mmo, reads=["merged", "Pn0", "Pn1", "Pn2", "Pn3", rk], writes=[f"ps{i}" for i in range(8)])
                if g < 3 and not P.dry:
                    for tt_ in range(4):
                        norm_B(tt_, V_GPM, 0, None, ("hT",), tt_)
                if not P.dry:
                    for tt0 in range(2):
                        tile0 = g * 4 + tt0
                        P.emit("sp", lambda e, tt0=tt0, tile0=tile0: e.dma_start(out=xtb[tt0][:], in_=src[s, tile0 * 128:(tile0 + 1) * 128, :]),
                               reads=[xkey(sname, s, tile0)], writes=[f"xt{tt0}"], dma=True)
                for kc in range(8):
                    wout_kc(kc)

                def wout_tile(tt):
                    if True:
                        tile = g * 4 + tt
                        pbs = [tt * 2, tt * 2 + 1]
                        xt = xtb[tt % 2]
                        kx = f"xt{tt % 2}"
                        if tt >= 2:
                            P.emit("sp", lambda e, xt=xt, tile=tile: e.dma_start(out=xt[:], in_=src[s, tile * 128:(tile + 1) * 128, :]),
                                   reads=[xkey(sname, s, tile)], writes=[kx], dma=True)
                        for half in range(2):
                            pb = pbs[half]

                            P.emit("act", lambda e, pb=pb, half=half: e.activation(
                                out=junk[:, 0:512], in_=ps[pb][:], func=AF.Square, accum_out=stat[:, 4 + half:5 + half]),
                                reads=[f"ps{pb}"], writes=["junk", f"pss{half}"])
                            P.emit("dve", lambda e, pb=pb, half=half: e.tensor_tensor(
                                out=tmpn[:, half * 512:(half + 1) * 512], in0=ps[pb][:], in1=gbp[:, 0, half * 512:(half + 1) * 512],
                                op=ALU.mult), reads=[f"ps{pb}"], writes=["tmpn"])
                        P.emit("dve", lambda e: e.tensor_tensor(out=stat[:, 6:7], in0=stat[:, 4:5], in1=stat[:, 5:6], op=ALU.add),
                               reads=["pss0", "pss1"], writes=["pss"])
                        rms_rstd(stat[:, 6:7], 0.25 / D, stat[:, 7:8], ["pss"], "prs")
                        P.emit("dve", lambda e: e.tensor_scalar(out=stat[:, 7:8], in0=stat[:, 7:8], scalar1=0.5, scalar2=None,
                                                                op0=ALU.mult), reads=["prs"], writes=["prs"])
                        P.emit("dve", lambda e, xt=xt: e.scalar_tensor_tensor(out=xt[:], in0=tmpn[:], scalar=stat[:, 7:8], in1=xt[:],
                                                                               op0=ALU.mult, op1=ALU.add),
                               reads=["tmpn", "prs", kx], writes=[kx])
                        P.emit("sp", lambda e, xt=xt, tile=tile: e.dma_start(out=dst[s, tile * 128:(tile + 1) * 128, :], in_=xt[:]),
                               reads=[kx], writes=[xkey(dname, s, tile)], dma=True)
                if not P.dry:
                    for tt in range(4):
                        wout_tile(tt)

        hrot = Rot((4, 5, 6))
        hcrot = Rot(range(8))

        def phase3(l, s, src, sname, dst, dname, final):
            for g in range(4):
                phase3_group(l, s, src, sname, dst, dname, g)

        def halo_A(s, src, sname, g):
            P.emit("pool", lambda e: e.memset(xh[:], 0.0), writes=["xh"])
            if g > 0:
                tl = g * 512 - 1
                P.emit("sp", lambda e: e.dma_start(out=xh[0:1, :], in_=src[s, tl:tl + 1, :]),
                       reads=[xkey(sname, s, tl // 128)], writes=["xh"], dma=True)
            if g < 3:
                tr_ = (g + 1) * 512
                P.emit("sp", lambda e: e.dma_start(out=xh[1:2, :], in_=src[s, tr_:tr_ + 1, :]),
                       reads=[xkey(sname, s, tr_ // 128)], writes=["xh"], dma=True)
            P.emit("act", lambda e: e.activation(out=junk[0:2, :], in_=xh[:], func=AF.Square, accum_out=stat[0:2, 8:9]),
                   reads=["xh"], writes=["junk", "hss"])
            rms_rstd(stat[0:2, 8:9], 1.0 / D, stat[0:2, 9:10], ["hss"], "hrs")
            P.emit("dve", lambda e: e.tensor_scalar(out=xnh[:], in0=xh[:], scalar1=stat[0:2, 9:10], scalar2=None, op0=ALU.mult),
                   reads=["xh", "hrs"], writes=["xnh"])

        def halo_B():
            def trh(e):
                for c in range(8):
                    ins = e.transpose(out=psT[:, c * 2:(c + 1) * 2], in_=xnh[0:2, c * 128:(c + 1) * 128], identity=ident[0:2, 0:2])
                return ins
            P.emit("pe", trh, reads=["xnh"], writes=["ps7"])
            P.emit("dve", lambda e: e.tensor_tensor(
                out=hT[:, :, 0:514:513], in0=psT[:, 0:16].rearrange("p (c t) -> p c t", c=8),
                in1=vecs[:, V_GPF:V_GPF + 8].unsqueeze(2).to_broadcast([128, 8, 2]), op=ALU.mult),
                reads=["ps7"], writes=["hT"])

        fb3 = list(fb) + [diff_T[:, i // 2, (i % 2) * 1024:(i % 2 + 1) * 1024].bitcast(F32) for i in range(8)]
        fb3rot = Rot(range(16))
        xr = [KC_T[:, i, :].bitcast(F32) for i in range(4)]

        def phase3_group(l, s, src, sname, dst, dname, g):
            if True:
                if g == 0:
                    norm_group(src, sname, s, g, V_GPF, 1)
                    halo_A(s, src, sname, g)
                    halo_B()
                urot = Rot((0, 1, 2, 3))

                def up_chunk(i, which, obuf):
                    if True:
                        col = which * 22 + i
                        rb, rk = ws.next(l, P3_BASE + 2 * i + which)
                        if P.dry:
                            return
                        pb = urot.next()
                        hi_ = hrot.next()
                        w = rb.rearrange("p (kc e) -> p kc e", kc=8)
                        psh = ps[hi_][:, 0:2]

                        def mmu(e, w=w, pb=pb, psh=psh):
                            for kc in range(8):
                                e.matmul(ps[pb][:], lhsT=w[:, kc, :], rhs=hT[:, kc, 1:513], start=(kc == 0), stop=(kc == 7))
                            for kc in range(8):
                                ins = e.matmul(psh, lhsT=w[:, kc, :], rhs=hT[:, kc, 0:514:513], start=(kc == 0), stop=(kc == 7))
                            return ins
                        P.emit("pe", mmu, reads=[rk, "hT"], writes=[f"ps{pb}", f"ps{hi_}"])
                        hc = hcrot.next()
                        hsb = halo_sb[:, hc * 2:hc * 2 + 2]
                        P.emit("act", lambda e, hsb=hsb, psh=psh: e.activation(out=hsb, in_=psh, func=AF.Copy), reads=[f"ps{hi_}"],
                               writes=[f"hsb{hc}"])
                        oi = fb3rot.next()
                        o2i = fb3rot.next()
                        O = fb3[oi]
                        O2 = fb3[o2i]
                        ok = f"fb{oi}"
                        ok2 = f"fb{o2i}"
                        w0 = vecs[:, V_CW0 + col:V_CW0 + col + 1]
                        w1 = vecs[:, V_CW1 + col:V_CW1 + col + 1]
                        w2 = vecs[:, V_CW2 + col:V_CW2 + col + 1]
                        cb = vecs[:, V_CB + col:V_CB + col + 1]
                        P.emit("act", lambda e, O=O, pb=pb, w1=w1, cb=cb: e.activation(out=O[:], in_=ps[pb][:], func=AF.Identity,
                                                                                        bias=cb, scale=w1),
                               reads=[f"ps{pb}"], writes=[ok])
                        P.emit("act", lambda e, O2=O2, pb=pb, w0=w0: e.activation(out=O2[:], in_=ps[pb][:], func=AF.Copy, scale=w0),
                               reads=[f"ps{pb}"], writes=[ok2])
                        P.emit("dve", lambda e, O=O, pb=pb, w2=w2: e.scalar_tensor_tensor(
                            out=O[:, 0:511], in0=ps[pb][:, 1:512], scalar=w2, in1=O[:, 0:511], op0=ALU.mult, op1=ALU.add),
                            reads=[f"ps{pb}", ok], writes=[ok])
                        P.emit("dve", lambda e, O=O, hsb=hsb, w2=w2: e.scalar_tensor_tensor(
                            out=O[:, 511:512], in0=hsb[:, 1:2], scalar=w2, in1=O[:, 511:512], op0=ALU.mult, op1=ALU.add),
                            reads=[f"hsb{hc}", ok], writes=[ok])
                        P.emit("dve", lambda e, O=O, hsb=hsb, w0=w0: e.scalar_tensor_tensor(
                            out=O[:, 0:1], in0=hsb[:, 0:1], scalar=w0, in1=O[:, 0:1], op0=ALU.mult, op1=ALU.add),
                            reads=[f"hsb{hc}", ok], writes=[ok])
                        P.emit("pool", lambda e, O=O, O2=O2: e.tensor_tensor(out=O[:, 1:512], in0=O[:, 1:512], in1=O2[:, 0:511],
                                                                              op=ALU.add), reads=[ok, ok2], writes=[ok])
                        obuf.append((O, ok))

                def up_pair(i):
                    obuf = []
                    for which in range(2):
                        up_chunk(i, which, obuf)
                    return obuf

                def up_finish(i, obuf):
                    (Ov, okv), (Og, okg) = obuf
                    P.emit("act", lambda e, Og=Og: e.activation(out=Og[:], in_=Og[:], func=AF.Gelu_apprx_tanh), reads=[okg], writes=[okg])
                    P.emit("dve", lambda e, Ov=Ov, Og=Og, i=i: e.tensor_tensor(out=act_T[:, i, :], in0=Og[:], in1=Ov[:], op=ALU.mult),
                           reads=[okv, okg], writes=["act_T"])
                prev = None
                for i in range(22):
                    cur_ = up_pair(i)
                    if prev is not None and not P.dry:
                        up_finish(i - 1, prev)
                    prev = cur_
                if not P.dry:
                    up_finish(21, prev)
                if not P.dry:
                    for tt0 in range(4):
                        tile0 = g * 4 + tt0
                        P.emit("sp", lambda e, tt0=tt0, tile0=tile0: e.dma_start(out=xr[tt0], in_=src[s, tile0 * 128:(tile0 + 1) * 128, :]),
                               reads=[xkey(sname, s, tile0)], writes=[f"xr{tt0}"], dma=True)
                sched = {}
                if g < 3 and not P.dry:
                    gn = g + 1
                    norm_A(src, sname, s, gn, 0)
                    norm_A(src, sname, s, gn, 1)
                    sched = {3: [lambda: norm_B(0, V_GPF, 1), lambda: norm_A(src, sname, s, gn, 2)],
                             6: [lambda: norm_B(1, V_GPF, 1), lambda: norm_A(src, sname, s, gn, 3)],
                             9: [lambda: norm_B(2, V_GPF, 1)],
                             13: [lambda: norm_B(3, V_GPF, 1), lambda: halo_A(s, src, sname, gn)],
                             17: [lambda: halo_B()]}
                nop = 0
                for half in range(2):
                    for j2 in range(11):
                        rb, rk = ws.next(l, P3_BASE + 44 + half * 11 + j2)
                        if P.dry:
                            continue

                        def mmd(e, rb=rb, j2=j2):
                            for j in range(2):
                                i = 2 * j2 + j
                                for tt in range(4):
                                    ins = e.matmul(ps[tt][:], lhsT=act_T[:, i, tt * 128:(tt + 1) * 128], rhs=rb[:, j * 512:(j + 1) * 512],
                                                   start=(i == 0), stop=(i == 21))
                            return ins
                        P.emit("pe", mmd, reads=[rk, "act_T"], writes=["ps0", "ps1", "ps2", "ps3"])
                        for f_ in sched.get(nop, ()):
                            f_()
                        nop += 1
                    if P.dry:
                        continue
                    for tt in range(4):
                        P.emit("act", lambda e, tt=tt, half=half: e.activation(
                            out=junk[:, 0:512], in_=ps[tt][:], func=AF.Square, accum_out=stat[:, 10 + tt * 2 + half:11 + tt * 2 + half]),
                            reads=[f"ps{tt}"], writes=["junk", f"dss{tt}_{half}"])
                        P.emit("dve", lambda e, tt=tt, half=half: e.tensor_tensor(
                            out=f_sb[tt][:, half * 512:(half + 1) * 512], in0=ps[tt][:], in1=gbp[:, 1, half * 512:(half + 1) * 512],
                            op=ALU.mult), reads=[f"ps{tt}"], writes=[f"f_sb{tt}"])
                if P.dry:
                    return

                def tail_tile(tt):
                    tile = g * 4 + tt
                    a = 10 + tt * 2
                    P.emit("dve", lambda e: e.tensor_tensor(out=stat[:, 6:7], in0=stat[:, a:a + 1], in1=stat[:, a + 1:a + 2],
                                                            op=ALU.add), reads=[f"dss{tt}_0", f"dss{tt}_1"], writes=["pss"])
                    rms_rstd(stat[:, 6:7], 1.0 / D, stat[:, 7:8], ["pss"], "prs")
                    P.emit("dve", lambda e: e.scalar_tensor_tensor(out=f_sb[tt][:], in0=f_sb[tt][:], scalar=stat[:, 7:8],
                                                                   in1=xr[tt], op0=ALU.mult, op1=ALU.add),
                           reads=["prs", f"xr{tt}"], writes=[f"f_sb{tt}"])
                    P.emit("sp", lambda e: e.dma_start(out=dst[s, tile * 128:(tile + 1) * 128, :], in_=f_sb[tt][:]),
                           reads=[f"f_sb{tt}"], writes=[xkey(dname, s, tile)], dma=True)
                for tt in range(4):
                    tail_tile(tt)

        def body():
            if not P.dry:
                load_consts()
            for l in range(L):
                if not P.dry:
                    load_layer_consts(l)
                src, sname = (x_in, "in") if l == 0 else (xsB, "B")
                last = (l == L - 1)
                dstf, dfname = (out, "out") if last else (xsB, "B")
                for s in range(NSEQ):
                    if 1 in phases:
                        phase1(l, s, src, sname)
                        P.barrier()
                    if 2 in phases:
                        phase2(l, s, src, sname, xsA, "A")
                        P.barrier()
                    if 3 in phases:
                        phase3(l, s, xsA if 2 in phases else src, "A", dstf, dfname, last)
                        P.barrier()

        P.dry = True
        body()
        P.dry = False
        body()
        P.finish(block, sems)
        if os.environ.get("KDBG_LOG"):
            with open(os.environ["KDBG_LOG"], "w") as fh:
                for r in P.log:
                    fh.write("%d %s %d\n" % r)
    return nc


_CACHE = {}


def kernel(x, norm_mix_pre, norm_mix_post, norm_ffn_pre, norm_ffn_post, w_in, b_gate, qk_norm_q, qk_norm_k,
           w_pool, pool_scale, rpb, w_branch, w_out, w_up, conv_w, conv_b, w_down):
    f = lambda a: np.ascontiguousarray(np.asarray(a, dtype=np.float32))
    x = f(x)
    L = 2
    wslots = np.stack([build_wslots(f(w_in[l]), f(w_pool[l]), f(w_branch[l]), f(w_out[l]), f(w_up[l]), f(w_down[l]))
                       for l in range(L)])
    vecs = np.stack([build_vecs(f(norm_mix_pre[l]), f(norm_ffn_pre[l]), f(b_gate[l]), f(conv_w[l]), f(conv_b[l]),
                                f(pool_scale[l]), f(qk_norm_q[l]), f(qk_norm_k[l])) for l in range(L)])
    gbp = np.stack([np.stack([f(norm_mix_post[l]), f(norm_ffn_post[l])]) for l in range(L)])
    natab = np.stack([build_natab(f(rpb[l])) for l in range(L)])
    cst, rope = build_consts()
    if "nc" not in _CACHE:
        _CACHE["nc"] = build_nc(L=2, NSEQ=2)
    nc = _CACHE["nc"]
    n = 8
    in_maps = []
    for c in range(n):
        in_maps.append({"x": np.ascontiguousarray(x[2 * c:2 * c + 2]), "wslots": wslots, "vecs": vecs, "gbp": gbp,
                        "natab": natab, "cst": cst, "rope": rope})
    res = run_bass_kernel_spmd(nc, in_maps, core_ids=list(range(n)))
    return np.concatenate([r["out"] for r in res.results], axis=0)
```

```python
import os
import numpy as np
from contextlib import ExitStack
import concourse.bass as bass
import concourse.mybir as mybir
from concourse.bass_utils import run_bass_kernel_spmd

F32 = mybir.dt.float32
BF16 = mybir.dt.bfloat16
AF = mybir.ActivationFunctionType
ALU = mybir.AluOpType

S = 2048
D = 1024
NSLOT = 140
P1_BASE, P2_BASE, P3_BASE = 0, 17, 74
NV = 222
V_GPM, V_GPF, V_BG, V_CW0, V_CW1, V_CW2, V_CB, V_PS, V_GQ, V_GK = 0, 8, 16, 40, 84, 128, 172, 216, 220, 221
NB = 8
NSTG = 2
LOOKAHEAD = 5
EPS = 1e-6


class Prog:
    ENG = ("pe", "act", "dve", "pool", "sp")

    def __init__(self, n_dma_sems=24, same_engine_sync=True):
        self.esem = {e: i for i, e in enumerate(self.ENG)}
        self.ND = n_dma_sems
        self.dbase = len(self.ENG)
        self.ecnt = {e: 0 for e in self.ENG}
        self.dcnt = [0] * self.ND
        self.dnext = 0
        self.seen = {e: {} for e in self.ENG}
        self.last_w = {}
        self.readers = {}
        self.ops = {e: [] for e in self.ENG}
        self.ses = same_engine_sync
        self.dry = False
        self.ps_last = {}
        self.nemit = 0
        self.maxops = int(os.environ.get("KDBG_MAXOPS", "1000000000"))
        self.log = []

    def emit(self, eng, fn, reads=(), writes=(), dma=False):
        if self.dry:
            return
        self.nemit += 1
        if self.nemit > self.maxops:
            return
        if os.environ.get("KDBG_LOG"):
            import inspect
            fr = inspect.stack()[1]
            self.log.append((self.nemit, eng, fr.lineno))
        waits = {}

        def need(tok):
            if tok is None:
                return
            s, v = tok
            if (not self.ses) and (not dma) and s == self.esem[eng]:
                return
            if self.seen[eng].get(s, 0) >= v:
                return
            if waits.get(s, 0) < v:
                waits[s] = v

        reads = list(reads)
        psr = [k for k in reads if k.startswith("ps") and k[2:].isdigit() and k not in writes]
        writes = list(writes) + psr
        reads = [k for k in reads if k not in writes]
        for k in reads:
            need(self.last_w.get(k))
        for k in writes:
            if k in psr and self.ps_last.get(k) == (eng, "r"):
                continue
            need(self.last_w.get(k))
            for t in self.readers.get(k, ()):
                need(t)
        for k in writes:
            if k.startswith("ps") and k[2:].isdigit():
                self.ps_last[k] = (eng, "r" if k in psr else "w")
        if dma:
            i = self.dnext
            self.dnext = (i + 1) % self.ND
            if self.dcnt[i] > 0:
                need((self.dbase + i, self.dcnt[i]))
            self.dcnt[i] += 16
            tok = (self.dbase + i, self.dcnt[i])
            amt = 16
        else:
            self.ecnt[eng] += 1
            tok = (self.esem[eng], self.ecnt[eng])
            amt = 1
        for s, v in waits.items():
            self.seen[eng][s] = v
        self.ops[eng].append((list(waits.items()), fn, tok[0], amt, False))
        for k in reads:
            self.readers.setdefault(k, []).append(tok)
        for k in writes:
            self.last_w[k] = tok
            self.readers[k] = []
        return tok

    def emit_ring_load(self, fn, b):
        if self.dry:
            return
        eng = "pool"
        key = f"ring{b}"
        sidx = self.dbase + self.ND + b
        waits = {}
        for tok in [self.last_w.get(key)] + list(self.readers.get(key, ())):
            if tok is None:
                continue
            s, v = tok
            if s == sidx:
                waits[s] = v
                continue
            if self.seen[eng].get(s, 0) >= v:
                continue
            if waits.get(s, 0) < v:
                waits[s] = v
        for s, v in waits.items():
            self.seen[eng][s] = v
        for e in self.ENG:
            self.seen[e].pop(sidx, None)
        self.ops[eng].append((list(waits.items()), fn, sidx, 16, True))
        self.last_w[key] = (sidx, 16)
        self.readers[key] = []

    def barrier(self):
        if self.dry:
            return
        allw = []
        for e in self.ENG:
            if self.ecnt[e] > 0:
                allw.append((self.esem[e], self.ecnt[e]))
        for i in range(self.ND):
            if self.dcnt[i] > 0:
                allw.append((self.dbase + i, self.dcnt[i]))
        for e in self.ENG:
            ws = []
            for s, v in allw:
                if self.seen[e].get(s, 0) >= v:
                    continue
                self.seen[e][s] = v
                ws.append((s, v))
            if ws:
                self.ops[e].append((ws, None, None, 0, False))
        self.ps_last = {}
        self.last_w = {k: v for k, v in self.last_w.items() if k.startswith("ring")}
        self.readers = {k: v for k, v in self.readers.items() if k.startswith("ring")}

    def finish(self, block, sems):
        self.barrier()

        def replay(ename):
            def run(eng):
                for waits, fn, s, amt, clr in self.ops[ename]:
                    for ws, wv in waits:
                        eng.wait_ge(sems[ws], wv)
                    if fn is not None:
                        if clr:
                            eng.sem_clear(sems[s])
                        ins = fn(eng)
                        ins.then_inc(sems[s], amt)
            return run

        block.tensor(replay("pe"))
        block.scalar(replay("act"))
        block.vector(replay("dve"))
        block.gpsimd(replay("pool"))
        block.sync(replay("sp"))


class Rot:
    def __init__(self, items):
        self.items = list(items)
        self.i = 0

    def next(self):
        v = self.items[self.i % len(self.items)]
        self.i += 1
        return v


def _e_slot(W, cols):
    sub = W[:, cols]
    return sub.reshape(8, 128, 128).transpose(1, 0, 2).reshape(128, 1024)


def build_wslots(w_in, w_pool, w_branch, w_out, w_up, w_down):
    sl = np.zeros((NSLOT, 128, 1024), np.float32)
    k = 0
    ar = np.arange
    sl[k] = _e_slot(w_in, ar(512, 640)); k += 1
    for c in range(4):
        sl[k] = _e_slot(w_in, ar(1792 + c * 128, 1792 + (c + 1) * 128)); k += 1
    for c in range(4):
        sl[k] = _e_slot(w_in, ar(768 + c * 128, 768 + (c + 1) * 128)); k += 1
    for kc in range(8):
        sl[k, :, 0:128] = w_in[kc * 128:(kc + 1) * 128, 640:768]
        sl[k, :, 128:640] = w_in[kc * 128:(kc + 1) * 128, 2304:2816]
        k += 1
    assert k == P2_BASE
    for c in range(4):
        cols = np.concatenate([ar(c * 64, c * 64 + 64), ar((4 + c) * 64, (4 + c) * 64 + 64)])
        sl[k] = _e_slot(w_in, cols); k += 1
    for c in range(4):
        sl[k] = _e_slot(w_in, ar(1280 + c * 128, 1280 + (c + 1) * 128)); k += 1
    for g in range(4):
        sl[k, :, g * 128:(g + 1) * 128] = w_pool[g]
    k += 1
    perm0 = np.concatenate([np.concatenate([ar(c * 64, c * 64 + 64), ar((4 + c) * 64, (4 + c) * 64 + 64)])
                            for c in range(4)])
    wb = [w_branch[0][perm0], w_branch[1], w_branch[2]]
    for dc in range(8):
        for n in range(3):
            sl[k] = _e_slot(w_in, ar(2816 + n * 1024 + dc * 128, 2816 + n * 1024 + (dc + 1) * 128)); k += 1
        for n in range(3):
            blk = wb[n][:, dc * 128:(dc + 1) * 128].reshape(4, 128, 128).transpose(1, 0, 2).reshape(128, 512)
            if n < 2:
                sl[k, :, n * 512:(n + 1) * 512] = blk
            else:
                sl[k + 1, :, 0:512] = blk
        k += 2
    for kc in range(8):
        sl[k] = w_out[kc * 128:(kc + 1) * 128, :]; k += 1
    assert k == P3_BASE
    for i in range(22):
        sl[k] = _e_slot(w_up, ar(i * 128, (i + 1) * 128)); k += 1
        sl[k] = _e_slot(w_up, ar(2816 + i * 128, 2816 + (i + 1) * 128)); k += 1
    for half in range(2):
        for j2 in range(11):
            for j in range(2):
                i = 2 * j2 + j
                sl[k, :, j * 512:(j + 1) * 512] = w_down[i * 128:(i + 1) * 128, half * 512:(half + 1) * 512]
            k += 1
    assert k == NSLOT
    return sl


def build_vecs(norm_mix_pre, norm_ffn_pre, b_gate, conv_w, conv_b, pool_scale, gq, gk):
    v = np.zeros((128, NV), np.float32)
    v[:, V_GPM:V_GPM + 8] = norm_mix_pre.reshape(8, 128).T
    v[:, V_GPF:V_GPF + 8] = norm_ffn_pre.reshape(8, 128).T
    v[:, V_BG:V_BG + 24] = b_gate.reshape(24, 128).T
    for j, base in enumerate((V_CW0, V_CW1, V_CW2)):
        v[:, base:base + 44] = conv_w[j].reshape(44, 128).T
    v[:, V_CB:V_CB + 44] = conv_b.reshape(44, 128).T
    v[:, V_PS:V_PS + 4] = pool_scale.reshape(4, 128).T
    v[:, V_GQ] = np.concatenate([gq, gq])
    v[:, V_GK] = np.concatenate([gk, gk])
    return v


def build_natab(rpb):
    kc = np.arange(64)[:, None]
    c = np.arange(64)[None, :]
    cs = np.clip(c - 8, 0, 48)
    valid = (kc >= cs) & (kc < cs + 16)
    dci = np.clip(kc - c + 15, 0, 30)
    g = rpb[:, :, dci]
    g = np.where(valid[None, None], g, np.float32(-200.0)).astype(np.float32)
    g = g.transpose(2, 0, 1, 3)
    g = np.concatenate([g, g], axis=0)
    return np.ascontiguousarray(g.reshape(128, 8 * 15 * 64))


def build_consts():
    cst = np.zeros((128, 128 * 3 + 64), np.float32)
    cst[:, 0:128] = np.eye(128, dtype=np.float32)
    p = np.arange(128)
    d = p % 64
    partner = np.where(d % 32 < 16, p + 16, p - 16)
    cst[partner, 128 + p] = 1.0
    cst[:, 256:384] = (p[:, None] // 64 == p[None, :] // 64).astype(np.float32)
    for g, w in enumerate((2, 4, 8, 16)):
        for t in range(8):
            cntl = min(t + w - w // 2, S) - max(t - w // 2, 0)
            cst[:, 384 + g * 16 + t] = w / cntl
            tr = S - 8 + t
            cntr = min(tr + w - w // 2, S) - max(tr - w // 2, 0)
            cst[:, 384 + g * 16 + 8 + t] = w / cntr
    t = np.arange(S)
    row = (t // 64).astype(np.float32)
    col = (t % 64).astype(np.float32)
    freqs = (np.float32(10000.0) ** (-(np.arange(0, 32, 2, dtype=np.float32)) / np.float32(32))).astype(np.float32)
    rope = np.zeros((2, 128, S), np.float32)
    for pp in range(128):
        dd = pp % 64
        pos = row if dd < 32 else col
        j = dd % 16
        ang = (pos * freqs[j]).astype(np.float32)
        sgn = -1.0 if (dd % 32) < 16 else 1.0
        rope[0, pp] = np.cos(ang)
        rope[1, pp] = sgn * np.sin(ang)
    return cst, rope


def build_nc(L=2, NSEQ=2, same_engine_sync=True, phases=(1, 2, 3)):
    nc = bass.Bass("TRN2", target_bir_lowering=False)
    x_in = nc.dram_tensor("x", [NSEQ, S, D], F32, kind="ExternalInput").ap()
    out = nc.dram_tensor("out", [NSEQ, S, D], F32, kind="ExternalOutput").ap()
    xsA = nc.dram_tensor("xsA", [NSEQ, S, D], F32, kind="Internal").ap()
    xsB = nc.dram_tensor("xsB", [NSEQ, S, D], F32, kind="Internal").ap()
    wsl = nc.dram_tensor("wslots", [L, NSLOT, 128, 1024], F32, kind="ExternalInput").ap()
    wbf = nc.dram_tensor("wbf", [L, NSLOT, 128, 1024], BF16, kind="Internal").ap()
    vecs_d = nc.dram_tensor("vecs", [L, 128, NV], F32, kind="ExternalInput").ap()
    gbp_d = nc.dram_tensor("gbp", [L, 2, D], F32, kind="ExternalInput").ap()
    natab_d = nc.dram_tensor("natab", [L, 128, 7680], F32, kind="ExternalInput").ap()
    cst_d = nc.dram_tensor("cst", [128, 448], F32, kind="ExternalInput").ap()
    rope_d = nc.dram_tensor("rope", [2, 128, S], F32, kind="ExternalInput").ap()

    with ExitStack() as es:
        def T(name, shape, dt):
            return es.enter_context(nc.sbuf_tensor("sb_" + name, shape, dt))

        cst = T("cst", [128, 448], F32)
        ident = T("ident", [128, 128], BF16)
        vecs = T("vecs", [128, NV], F32)
        bhalf = T("bhalf", [128, 24], F32)
        gq8 = T("gq8", [128, 1], F32)
        gbp = T("gbp", [128, 2, D], F32)
        M2 = T("M2", [128, 8, 15, 64], BF16)
        ropeC = T("ropeC", [128, 512], F32)
        ropeS = T("ropeS", [128, 512], F32)
        KA_T = T("KA_T", [128, S], BF16)
        VA1 = T("VA1", [128, 16 * 192], BF16)
        KC_T = T("KC_T", [128, 4, S], BF16)
        VC1 = T("VC1", [128, 16 * 768], BF16)
        diff_T = T("diff_T", [128, 4, S], BF16)
        xtb = [T(f"xt{i}", [128, D], F32) for i in range(2)]
        xnb = [T(f"xn{i}", [128, D], BF16) for i in range(2)]
        hT = T("hT", [128, 8, 514], BF16)
        fb = [T(f"fb{i}", [128, 512], F32) for i in range(8)]
        junk = T("junk", [128, D], BF16)
        tmpn = T("tmpn", [128, D], F32)
        stat = T("stat", [128, 24], F32)
        halo_sb = T("halo_sb", [128, 16], F32)
        xh = T("xh", [2, D], F32)
        xnh = T("xnh", [2, D], BF16)
        ring = T("ring", [128, NB, 1024], BF16)
        stg2 = T("stg2", [128, NSTG, 1024], F32)
        stg = [stg2[:, i, :] for i in range(NSTG)]
        hT2 = stg2[:].rearrange("p a d -> p (a d)").bitcast(BF16).rearrange("p (c t) -> p c t", c=8)
        HT2_KEYS = ["stg0", "stg1", "hT2"]
        ARENA = 38912
        arena = T("arena", [128, ARENA // 2], BF16)

        def aview(off_b, nbytes, dt):
            a = arena[:, off_b // 2:(off_b + nbytes) // 2]
            return a if dt == BF16 else a.bitcast(dt)

        u_f = aview(0, 4 * 2064 * 4, F32).rearrange("p (g t) -> p g t", g=4)
        o = 0
        qAz = aview(o, 8192, BF16).rearrange("p (h t) -> p h t", h=8); o += 8192
        qC_T = aview(o, 4096, BF16).rearrange("p (c t) -> p c t", c=4); o += 4096
        y_T = [aview(o + n * 4096, 4096, BF16).rearrange("p (c t) -> p c t", c=4) for n in range(3)]; o += 12288
        Pn = [aview(o + i * 2048, 2048, BF16).rearrange("p (h q) -> p h q", h=8) for i in range(5)]
        merged_T = aview(o, 8192, BF16).rearrange("p (c t) -> p c t", c=8)
        o += 10240
        pTb = [aview(o + i * 1024, 1024, BF16) for i in range(4)]
        Pn2 = [aview(i * 2048, 2048, BF16).rearrange("p (h q) -> p h q", h=8) for i in range(4)] + \
              [aview(o, 2048, BF16).rearrange("p (h q) -> p h q", h=8)]
        Pn2_keys = [[f"PnB{i}"] for i in range(5)]
        o += 4096
        assert o <= ARENA
        act_T = aview(0, 22528, BF16).rearrange("p (i t) -> p i t", i=22)
        f_sb = [aview(22528 + i * 4096, 4096, F32) for i in range(4)]
        assert 22528 + 16384 <= ARENA

        ps = [es.enter_context(nc.psum_tensor(f"ps{i}", [128, 512], F32)) for i in range(8)]
        psT = ps[7][:].bitcast(BF16)
        NSEM = 5 + 24
        sems = [es.enter_context(nc.semaphore(f"s{i}")) for i in range(NSEM)]
        block = es.enter_context(nc.Block())
        P = Prog(n_dma_sems=24, same_engine_sync=same_engine_sync)

        identf = cst[:, 0:128]
        Rm_f = cst[:, 128:256]
        BD_f = cst[:, 256:384]
        edge = cst[:, 384:448].rearrange("p (g t) -> p g t", g=4)

        class WS:
            def __init__(self):
                self.reqs = []
                self.ptr = 0
                self.issued = 0
                self.converted = set()
                self.nconv = 0
                self.pending_store = {}

            def next(self, l, idx):
                if P.dry:
                    self.reqs.append((l, idx))
                    return None, None
                assert self.reqs[self.ptr] == (l, idx), (self.reqs[self.ptr], l, idx)
                while self.issued < min(len(self.reqs), self.ptr + 1 + LOOKAHEAD):
                    ll, ii = self.reqs[self.issued]
                    b = self.issued % NB
                    if (ll, ii) not in self.converted:
                        self.converted.add((ll, ii))
                        sg = self.nconv % NSTG
                        self.nconv += 1
                        P.emit("sp", (lambda e, ll=ll, ii=ii, sg=sg: e.dma_start(out=stg[sg], in_=wsl[ll, ii])),
                               writes=[f"stg{sg}"], dma=True)
                        if self.nconv % 2:
                            P.emit("act", (lambda e, b=b, sg=sg: e.activation(out=ring[:, b, :], in_=stg[sg], func=AF.Copy)),
                                   reads=[f"stg{sg}"], writes=[f"ring{b}"])
                        else:
                            P.emit("dve", (lambda e, b=b, sg=sg: e.tensor_copy(out=ring[:, b, :], in_=stg[sg])),
                                   reads=[f"stg{sg}"], writes=[f"ring{b}"])
                        self.pending_store[self.issued] = (ll, ii)
                    else:
                        P.emit("sp", (lambda e, ll=ll, ii=ii, b=b: e.dma_start(out=ring[:, b, :], in_=wbf[ll, ii])),
                               reads=[f"wbf{ll}_{ii}"], writes=[f"ring{b}"], dma=True)
                    self.issued += 1
                b = self.ptr % NB
                if self.ptr in self.pending_store:
                    ll, ii = self.pending_store.pop(self.ptr)
                    P.emit("sp", (lambda e, ll=ll, ii=ii, b=b: e.dma_start(out=wbf[ll, ii], in_=ring[:, b, :])),
                           reads=[f"ring{b}"], writes=[f"wbf{ll}_{ii}"], dma=True)
                self.ptr += 1
                return ring[:, b, :], f"ring{b}"

        ws = WS()

        def V1ap(t, nh, h, is_c):
            if is_c:
                c0 = t * 768 + (h // 2) * 192 + (h % 2) * 64
                return VC1[:, c0:c0 + 128]
            c0 = t * 192 + h * 64
            return VA1[:, c0:c0 + 128]

        def load_consts():
            P.emit("sp", lambda e: e.dma_start(out=cst[:], in_=cst_d[:, :]), writes=["cst"], dma=True)
            P.emit("dve", lambda e: e.tensor_copy(out=ident[:], in_=cst[:, 0:128]), reads=["cst"], writes=["ident"])
            P.emit("pool", lambda e: e.memset(VA1[:].rearrange("p (t a b) -> p t a b", a=3, b=64)[:, :, 1, :], 1.0), writes=["VA1"])
            P.emit("pool", lambda e: e.memset(VC1[:].rearrange("p (t a b) -> p t a b", a=3, b=64)[:, :, 1, :], 1.0), writes=["VC1"])
            P.barrier()

        def precast_weights(l):
            nst = 9
            stgs = [aview(i * 4096, 4096, F32) for i in range(nst)]
            engs = ("act", "dve", "act", "dve", "act", "dve", "act", "pool")
            for idx in range(NSLOT):
                sg = idx % nst
                b = idx % NB
                P.emit("sp", (lambda e, idx=idx, sg=sg: e.dma_start(out=stgs[sg][:], in_=wsl[l, idx])),
                       writes=[f"pstg{sg}"], dma=True)
                en = engs[idx % len(engs)]
                if en == "act":
                    P.emit("act", (lambda e, b=b, sg=sg: e.activation(out=ring[:, b, :], in_=stgs[sg][:], func=AF.Copy)),
                           reads=[f"pstg{sg}"], writes=[f"ring{b}"])
                else:
                    P.emit(en, (lambda e, b=b, sg=sg: e.tensor_copy(out=ring[:, b, :], in_=stgs[sg][:])),
                           reads=[f"pstg{sg}"], writes=[f"ring{b}"])
                P.emit("sp", (lambda e, idx=idx, b=b: e.dma_start(out=wbf[l, idx], in_=ring[:, b, :])),
                       reads=[f"ring{b}"], writes=[f"wbf{idx}"], dma=True)

        def load_layer_consts(l):
            P.barrier()
            P.emit("sp", lambda e: e.dma_start(out=vecs[:], in_=vecs_d[l]), writes=["vecs"], dma=True)
            P.emit("sp", lambda e: e.dma_start(out=gbp[:].rearrange("p a d -> p (a d)"),
                                                in_=gbp_d[l].rearrange("a d -> (a d)").partition_broadcast(128)),
                   writes=["gbp"], dma=True)
            P.emit("dve", lambda e: e.tensor_scalar(out=bhalf[:], in0=vecs[:, V_BG:V_BG + 24], scalar1=0.5, scalar2=None,
                                                    op0=ALU.mult), reads=["vecs"], writes=["bhalf"])
            P.emit("dve", lambda e: e.tensor_scalar(out=gq8[:], in0=vecs[:, V_GQ:V_GQ + 1], scalar1=0.125, scalar2=None,
                                                    op0=ALU.mult), reads=["vecs"], writes=["gq8"])
            M2f = M2[:].rearrange("p h r c -> p (h r c)")
            for i in range(8):
                P.emit("sp", lambda e, i=i: e.dma_start(out=tmpn[:, 0:960], in_=natab_d[l][:, i * 960:(i + 1) * 960]),
                       writes=["tmpn"], dma=True)
                P.emit("act", lambda e, i=i: e.activation(out=M2f[:, i * 960:(i + 1) * 960], in_=tmpn[:, 0:960], func=AF.Exp),
                       reads=["tmpn"], writes=["M2"])
            P.barrier()

        def xkey(buf, s, tile):
            return f"x{buf}_{s}_{tile}"

        def rms_rstd(ss_ap, scale, rstd_ap, rkeys, wkey):
            P.emit("act", lambda e: e.activation(out=rstd_ap, in_=ss_ap, func=AF.Ln, scale=scale, bias=EPS),
                   reads=rkeys, writes=[wkey])
            P.emit("act", lambda e: e.activation(out=rstd_ap, in_=rstd_ap, func=AF.Exp, scale=-0.5),
                   reads=[wkey], writes=[wkey])

        xn4 = [(xnb[0][:], "xn0"), (xnb[1][:], "xn1"),
               (tmpn[:].bitcast(BF16)[:, 0:1024], "tmpn"), (tmpn[:].bitcast(BF16)[:, 1024:2048], "tmpn")]

        def norm_A(src, sname, s, g, tt, xsel=None):
            tile = g * 4 + tt
            xt = xtb[tt % 2]
            xn, kn = (xnb[tt % 2][:], f"xn{tt % 2}") if xsel is None else (xn4[xsel][0], xn4[xsel][1])
            kx = f"xt{tt % 2}"
            ss = stat[:, tt % 2:tt % 2 + 1]
            rs = stat[:, 2 + tt % 2:3 + tt % 2]
            P.emit("sp", lambda e: e.dma_start(out=xt[:], in_=src[s, tile * 128:(tile + 1) * 128, :]),
                   reads=[xkey(sname, s, tile)], writes=[kx], dma=True)
            P.emit("act", lambda e: e.activation(out=junk[:], in_=xt[:], func=AF.Square, accum_out=ss),
                   reads=[kx], writes=["junk", f"ss{tt % 2}"])
            rms_rstd(ss, 1.0 / D, rs, [f"ss{tt % 2}"], f"rs{tt % 2}")
            P.emit("dve", lambda e: e.tensor_scalar(out=xn, in0=xt[:], scalar1=rs, scalar2=None, op0=ALU.mult),
                   reads=[kx, f"rs{tt % 2}"], writes=[kn])

        def norm_B(tt, gcol, col0, hbuf=None, hkeys=("hT",), xsel=None):
            xn, kn = (xnb[tt % 2][:], f"xn{tt % 2}") if xsel is None else (xn4[xsel][0], xn4[xsel][1])
            if hbuf is None:
                hbuf = hT

            def tr(e):
                for c in range(8):
                    ins = e.transpose(out=psT[:, c * 128:(c + 1) * 128], in_=xn[:, c * 128:(c + 1) * 128], identity=ident[:])
                return ins
            P.emit("pe", tr, reads=[kn], writes=["ps7"])
            P.emit("dve", lambda e: e.tensor_tensor(
                out=hbuf[:, :, col0 + tt * 128:col0 + (tt + 1) * 128], in0=psT.rearrange("p (c t) -> p c t", c=8),
                in1=vecs[:, gcol:gcol + 8].unsqueeze(2).to_broadcast([128, 8, 128]), op=ALU.mult),
                reads=["ps7"], writes=list(hkeys))

        def norm_group(src, sname, s, g, gcol, col0, hbuf=None, hkeys=("hT",)):
            for tt in range(4):
                norm_A(src, sname, s, g, tt)
                norm_B(tt, gcol, col0, hbuf, hkeys)

        def proj_e(l, idx, psb, col0=0, n=512, hbuf=None, hkeys=("hT",)):
            rb, rk = ws.next(l, idx)
            if P.dry:
                return
            w = rb.rearrange("p (kc e) -> p kc e", kc=8)
            if hbuf is None:
                hbuf = hT

            def mm(e):
                for kc in range(8):
                    ins = e.matmul(ps[psb][:, 0:n], lhsT=w[:, kc, :], rhs=hbuf[:, kc, col0:col0 + n],
                                   start=(kc == 0), stop=(kc == 7))
                return ins
            P.emit("pe", mm, reads=[rk] + list(hkeys), writes=[f"ps{psb}"])

        fbrot = Rot(range(8))

        def qk_pipeline(psb, gvec, out_ap, okey):
            a, b, c = fbrot.next(), fbrot.next(), fbrot.next()
            A, B, C = fb[a], fb[b], fb[c]
            ka, kb, kc_ = f"fb{a}", f"fb{b}", f"fb{c}"
            P.emit("act", lambda e: e.activation(out=A[:], in_=ps[psb][:], func=AF.Square), reads=[f"ps{psb}"], writes=[ka])
            P.emit("pe", lambda e: e.matmul(ps[4][:], lhsT=BD_f, rhs=A[:], start=True, stop=True), reads=[ka], writes=["ps4"])
            P.emit("act", lambda e: e.activation(out=B[:], in_=ps[4][:], func=AF.Ln, scale=1.0 / 64, bias=EPS),
                   reads=["ps4"], writes=[kb])
            P.emit("act", lambda e: e.activation(out=B[:], in_=B[:], func=AF.Exp, scale=-0.5), reads=[kb], writes=[kb])
            P.emit("dve", lambda e: e.scalar_tensor_tensor(out=C[:], in0=ps[psb][:], scalar=gvec, in1=B[:], op0=ALU.mult,
                                                           op1=ALU.mult), reads=[f"ps{psb}", kb], writes=[kc_])
            P.emit("pe", lambda e: e.matmul(ps[5][:], lhsT=Rm_f, rhs=C[:], start=True, stop=True), reads=[kc_], writes=["ps5"])
            P.emit("dve", lambda e: e.tensor_tensor(out=A[:], in0=C[:], in1=ropeC[:], op=ALU.mult), reads=[kc_, "rope"], writes=[ka])
            P.emit("dve", lambda e: e.tensor_tensor(out=B[:], in0=ps[5][:], in1=ropeS[:], op=ALU.mult), reads=["ps5", "rope"],
                   writes=[kb])
            if isinstance(out_ap, list):
                for (psl, oap) in out_ap:
                    P.emit("pool", lambda e, psl=psl, oap=oap: e.tensor_tensor(out=oap, in0=A[psl, :], in1=B[psl, :], op=ALU.add),
                           reads=[ka, kb], writes=[okey])
            else:
                P.emit("pool", lambda e: e.tensor_tensor(out=out_ap, in0=A[:], in1=B[:], op=ALU.add), reads=[ka, kb], writes=[okey])

        def load_rope(g):
            P.emit("sp", lambda e: e.dma_start(out=ropeC[:], in_=rope_d[0][:, g * 512:(g + 1) * 512]), writes=["rope"], dma=True)
            P.emit("sp", lambda e: e.dma_start(out=ropeS[:], in_=rope_d[1][:, g * 512:(g + 1) * 512]), writes=["rope"], dma=True)

        def phase1(l, s, src, sname):
            P.emit("pool", lambda e: e.memset(u_f[:, :, 0:8], 0.0), writes=["u_fpad"])
            P.emit("pool", lambda e: e.memset(u_f[:, :, 2056:2064], 0.0), writes=["u_fpad"])

            def p1_group(g):
                if g == 0:
                    load_rope(g)
                    norm_group(src, sname, s, g, V_GPM, 0)
                tsl = slice(g * 512, (g + 1) * 512)
                proj_e(l, P1_BASE + 0, 0)
                qk_pipeline(0, vecs[:, V_GK:V_GK + 1], KA_T[:, tsl], "KA_T")
                for c in range(4):
                    pb = 1 + c % 3
                    proj_e(l, P1_BASE + 1 + c, pb)
                    P.emit("act", lambda e, c=c, pb=pb: e.activation(out=KC_T[:, c, tsl], in_=ps[pb][:], func=AF.Copy),
                           reads=[f"ps{pb}"], writes=["KC_T"])
                for c in range(4):
                    pb = 1 + (c + 1) % 3
                    proj_e(l, P1_BASE + 5 + c, pb)
                    P.emit("dve", lambda e, c=c, pb=pb: e.tensor_copy(out=u_f[:, c, 8 + g * 512:8 + (g + 1) * 512], in_=ps[pb][:]),
                           reads=[f"ps{pb}"], writes=[f"u_f{g}"])
                if g < 3 and not P.dry:
                    load_rope(g + 1)
                    for tt in range(4):
                        norm_A(src, sname, s, g + 1, tt, tt)

                def v_kc(kc):
                    rb, rk = ws.next(l, P1_BASE + 9 + kc)
                    if P.dry:
                        return

                    def mmv(e):
                        for tt in range(4):
                            e.matmul(ps[4 + tt][:, 0:128], lhsT=hT[:, kc, tt * 128:(tt + 1) * 128], rhs=rb[:, 0:128],
                                     start=(kc == 0), stop=(kc == 7))
                            ins = e.matmul(ps[tt][:], lhsT=hT[:, kc, tt * 128:(tt + 1) * 128], rhs=rb[:, 128:640],
                                           start=(kc == 0), stop=(kc == 7))
                        return ins
                    P.emit("pe", mmv, reads=["hT", rk], writes=[f"ps{i}" for i in range(8)])
                for kc in range(8):
                    v_kc(kc)
                if P.dry:
                    return
                for tt in range(4):
                    tile = g * 4 + tt
                    P.emit("act", lambda e, tile=tile, tt=tt: e.activation(
                        out=VA1[:, tile * 192:(tile + 1) * 192].rearrange("p (a b) -> p a b", b=64)[:, 0:3:2, :],
                        in_=ps[4 + tt][:, 0:128].rearrange("p (a b) -> p a b", b=64), func=AF.Copy),
                        reads=[f"ps{4 + tt}"], writes=["VA1"])
                    P.emit("dve", lambda e, tile=tile, tt=tt: e.tensor_copy(
                        out=VC1[:, tile * 768:(tile + 1) * 768].rearrange("p (r a b) -> p r a b", a=3, b=64)[:, :, 0:3:2, :],
                        in_=ps[tt][:].rearrange("p (r a b) -> p r a b", a=2, b=64)), reads=[f"ps{tt}"], writes=["VC1"])
                if g < 3:
                    for tt in range(4):
                        norm_B(tt, V_GPM, 0, None, ("hT",), tt)
            for g in range(4):
                p1_group(g)
            if not P.dry:
                bufs = [(tmpn, "tmpn"), (xtb[0], "xt0")]
                rk_all = ["u_f0", "u_f1", "u_f2", "u_f3", "u_fpad"]
                for gi, w in enumerate((2, 4, 8, 16)):
                    for (ca, cb_) in ((0, 1008), (1008, 2016), (2016, 2048)):
                        pool_chunk(gi, w, ca, cb_, bufs, "dve", rk_all, "diff_T")

        def pool_chunk(gi, w, ca, cb_, bufs, eng, rkeys, dkey):
            W = cb_ - ca + 16
            assert W <= 1024
            U = u_f[:, gi, ca:ca + W]
            (cur, ck), (oth, ok) = bufs
            P.emit(eng, lambda e, cur=cur: e.tensor_tensor(out=cur[:, 1:W], in0=U[:, 0:W - 1], in1=U[:, 1:W], op=ALU.add),
                   reads=rkeys, writes=[ck])
            lo, hi, sh = 1, W, 1
            for _ in range(gi):
                nlo, nhi = lo + sh, hi - sh
                P.emit(eng, lambda e, cur=cur, oth=oth, nlo=nlo, nhi=nhi, sh=sh: e.tensor_tensor(
                    out=oth[:, nlo:nhi], in0=cur[:, nlo - sh:nhi - sh], in1=cur[:, nlo + sh:nhi + sh], op=ALU.add),
                    reads=[ck], writes=[ok])
                cur, oth, ck, ok = oth, cur, ok, ck
                lo, hi = nlo, nhi
                sh *= 2
            assert lo <= 8 and hi >= W - 8
            if ca == 0:
                P.emit(eng, lambda e, cur=cur: e.tensor_tensor(out=cur[:, 8:16], in0=cur[:, 8:16], in1=edge[:, gi, 0:8],
                                                               op=ALU.mult), reads=[ck], writes=[ck])
            if cb_ == S:
                P.emit(eng, lambda e, cur=cur: e.tensor_tensor(out=cur[:, W - 16:W - 8], in0=cur[:, W - 16:W - 8],
                                                               in1=edge[:, gi, 8:16], op=ALU.mult), reads=[ck], writes=[ck])
            P.emit("dve", lambda e, cur=cur: e.scalar_tensor_tensor(
                out=diff_T[:, gi, ca:cb_], in0=cur[:, 8:W - 8], scalar=1.0 / w, in1=U[:, 8:W - 8], op0=ALU.mult, op1=ALU.subtract),
                reads=[ck] + rkeys, writes=[dkey])

        def phase2(l, s, src, sname, dst, dname):
            for g in range(4):
                phase2_group(l, s, src, sname, dst, dname, g)

        def p2_prep_q():
            P.emit("pool", lambda e: e.memset(qAz[64:128, 0:4, :], 0.0), writes=["qA_T", "pT0", "pT1"] + [f"PnB{i}" for i in range(5)])
            P.emit("pool", lambda e: e.memset(qAz[0:64, 4:8, :], 0.0), writes=["qA_T"])

        def phase2_group(l, s, src, sname, dst, dname, g):
            hb, hk = hT, ("hT",)
            if True:
                if g == 0:
                    load_rope(g)
                    p2_prep_q()
                    norm_group(src, sname, s, g, V_GPM, 0, hb, hk)
                qreq = [P2_BASE + c for c in range(4)] + [P2_BASE + 4 + c for c in range(4)]
                for c in range(4):
                    pb = (2 * c) % 4
                    proj_e(l, P2_BASE + c, pb, hbuf=hb, hkeys=hk)
                    qk_pipeline(pb, gq8[:, 0:1], [(slice(0, 64), qAz[0:64, c, :]), (slice(64, 128), qAz[64:128, 4 + c, :])], "qA_T")
                    pb2 = (2 * c + 1) % 4
                    proj_e(l, P2_BASE + 4 + c, pb2, hbuf=hb, hkeys=hk)
                    P.emit("act", lambda e, c=c, pb2=pb2: e.activation(out=qC_T[:, c, :], in_=ps[pb2][:], func=AF.Copy, scale=0.125),
                           reads=[f"ps{pb2}"], writes=["qC_T"])
                items = [(h, st) for h in range(8) for st in range(16)]
                srot = Rot((0, 1, 2))
                prot = Rot(range(4))
                pend = []

                def pv(h, st, pi):
                    ob = 3 + h % 2
                    P.emit("pe", lambda e: e.matmul(ps[ob][:], lhsT=V1ap(st, 2, h // 4, False), rhs=pTb[pi][:],
                                                    start=(st == 0), stop=(st == 15)),
                           reads=[f"pT{pi}", "VA1"], writes=[f"ps{ob}"])
                    if st == 15:
                        r = fbrot.next()
                        c, half = h % 4, h // 4
                        ns = slice(half * 64, half * 64 + 64)
                        ds = slice((1 - half) * 64, (1 - half) * 64 + 64)
                        P.emit("dve", lambda e: e.reciprocal(out=fb[r][ns, :], in_=ps[ob][ds, :]), reads=[f"ps{ob}"],
                               writes=[f"fb{r}"])
                        P.emit("dve", lambda e: e.tensor_tensor(out=y_T[0][ns, c, :], in0=ps[ob][ns, :],
                                                                in1=fb[r][ns, :], op=ALU.mult),
                               reads=[f"ps{ob}", f"fb{r}"], writes=["y0"])
                def run_attention(hook):
                    for (h, st) in items:
                        sb = srot.next()
                        pi = prot.next()
                        P.emit("pe", lambda e, h=h, st=st, sb=sb: e.matmul(
                            ps[sb][:], lhsT=KA_T[:, st * 128:(st + 1) * 128], rhs=qAz[:, h, :], start=True, stop=True),
                            reads=["KA_T", "qA_T"], writes=[f"ps{sb}"])
                        P.emit("act", lambda e, sb=sb, pi=pi: e.activation(out=pTb[pi][:], in_=ps[sb][:], func=AF.Exp),
                               reads=[f"ps{sb}"], writes=[f"pT{pi}"])
                        pend.append((h, st, pi))
                        if len(pend) > 2:
                            pv(*pend.pop(0))
                        if st == 15:
                            hook(h)
                    while pend:
                        pv(*pend.pop(0))
                def na_tile(jl):
                    J = g * 4 + jl
                    r0, r1 = 2 * J, 2 * J + 1
                    rs0, rs1 = min(max(r0 - 4, 0), 24), min(max(r1 - 4, 0), 24)
                    tiles = list(range(rs0 // 2, (rs1 + 7) // 2 + 1))
                    pset, pkeys = Pn, [[f"Pn{i}"] for i in range(5)]
                    qsl = slice(jl * 128, (jl + 1) * 128)
                    def na_ktile(i, t):
                        pn = pset[i]
                        pk = pkeys[i][-1]
                        pks = pkeys[i]
                        for par in range(2):
                            sb = srot.next()

                            def mmq(e, t=t, par=par, sb=sb):
                                hs = slice(par * 64, par * 64 + 64)
                                for a4 in range(4):
                                    ins = e.matmul(ps[sb][:, a4 * 128:(a4 + 1) * 128], lhsT=KC_T[hs, a4, t * 128:(t + 1) * 128],
                                                   rhs=qC_T[hs, a4, qsl], start=True, stop=True)
                                return ins
                            P.emit("pe", mmq, reads=["KC_T", "qC_T"], writes=[f"ps{sb}"])
                            P.emit("act", lambda e, pn=pn, par=par, sb=sb: e.activation(
                                out=pn[:, par:8:2, :], in_=ps[sb][:].rearrange("p (h q) -> p h q", h=4), func=AF.Exp),
                                reads=[f"ps{sb}"], writes=pks)
                        for kl in range(2):
                            for ql in range(2):
                                kr, r = 2 * t + kl, 2 * J + ql
                                rs = min(max(r - 4, 0), 24)
                                ksl = slice(kl * 64, kl * 64 + 64)
                                qs2 = slice(ql * 64, ql * 64 + 64)
                                if rs <= kr < rs + 8:
                                    dri = kr - r + 7
                                    P.emit("dve", lambda e, pn=pn, ksl=ksl, qs2=qs2, dri=dri: e.tensor_tensor(
                                        out=pn[ksl, :, qs2], in0=pn[ksl, :, qs2], in1=M2[ksl, :, dri, :], op=ALU.mult),
                                        reads=pks, writes=pks)
                                else:
                                    P.emit("pool", lambda e, pn=pn, ksl=ksl, qs2=qs2: e.memset(pn[ksl, :, qs2], 0.0),
                                           reads=pks, writes=pks)
                    for i, t in enumerate(tiles):
                        na_ktile(i, t)
                    return (tiles, pset, pkeys, qsl)

                def na_stage2(ctx):
                    tiles, pset, pkeys, qsl = ctx
                    for hg in range(2):
                        ob = 5 + hg

                        def mmpv(e, hg=hg, ob=ob):
                            for hh in range(4):
                                h = hg * 4 + hh
                                for i, t in enumerate(tiles):
                                    ins = e.matmul(ps[ob][:, hh * 128:(hh + 1) * 128], lhsT=V1ap(t, 8, h, True),
                                                   rhs=pset[i][:, h, :], start=(i == 0), stop=(i == len(tiles) - 1))
                            return ins
                        P.emit("pe", mmpv, reads=[k for i in range(len(tiles)) for k in pkeys[i]] + ["VC1"], writes=[f"ps{ob}"])
                        r = fbrot.next()
                        for par in range(2):
                            ns = slice(par * 64, par * 64 + 64)
                            ds = slice((1 - par) * 64, (1 - par) * 64 + 64)
                            P.emit("dve", lambda e, r=r, ob=ob, par=par, ns=ns, ds=ds: e.reciprocal(
                                out=fb[r][ns, :].rearrange("p (a b q) -> p a b q", a=2, b=2)[:, :, par, :],
                                in_=ps[ob][ds, :].rearrange("p (a b q) -> p a b q", a=2, b=2)[:, :, par, :]),
                                reads=[f"ps{ob}"], writes=[f"fb{r}"])
                            P.emit("dve", lambda e, r=r, ob=ob, par=par, hg=hg, ns=ns: e.tensor_tensor(
                                out=y_T[2][ns, hg * 2:(hg + 1) * 2, qsl],
                                in0=ps[ob][ns, :].rearrange("p (a b q) -> p a b q", a=2, b=2)[:, :, par, :],
                                in1=fb[r][ns, :].rearrange("p (a b q) -> p a b q", a=2, b=2)[:, :, par, :], op=ALU.mult),
                                reads=[f"ps{ob}", f"fb{r}"], writes=["y2"])
                ctxs = {}

                def na_hook(h):
                    if h % 2 == 0:
                        ctxs[h // 2] = na_tile(h // 2)
                    else:
                        na_stage2(ctxs.pop(h // 2))
                run_attention(na_hook)
                rb, rk = ws.next(l, P2_BASE + 8)
                if not P.dry:
                    for gi in range(4):
                        pb = gi % 3
                        P.emit("pe", lambda e, gi=gi, pb=pb: e.matmul(ps[pb][:], lhsT=rb[:, gi * 128:(gi + 1) * 128],
                                                                      rhs=diff_T[:, gi, g * 512:(g + 1) * 512], start=True, stop=True),
                               reads=[rk, "diff_T"], writes=[f"ps{pb}"])
                        P.emit("dve", lambda e, gi=gi, pb=pb: e.tensor_scalar(out=y_T[1][:, gi, :], in0=ps[pb][:],
                                                                              scalar1=vecs[:, V_PS + gi:V_PS + gi + 1], scalar2=None,
                                                                              op0=ALU.mult), reads=[f"ps{pb}"], writes=["y1"])
                grot = Rot((0, 1, 2))
                zrot = Rot((3, 4, 5))
                def merge_dc(dc):
                    base = P2_BASE + 9 + dc * 5
                    acc_i = fbrot.next()
                    brs = {}
                    for n in range(3):
                        merge_n(dc, n, base, acc_i, brs)

                def merge_n(dc, n, base, acc_i, brs):
                    if True:
                        gb_ = grot.next()
                        zb = zrot.next()
                        proj_e(l, base + n, gb_, hbuf=hb, hkeys=hk)
                        tg = fbrot.next()
                        P.emit("act", lambda e, gb_=gb_, tg=tg, n=n, dc=dc: e.activation(
                            out=fb[tg][:], in_=ps[gb_][:], func=AF.Tanh, scale=0.5, bias=bhalf[:, n * 8 + dc:n * 8 + dc + 1]),
                            reads=[f"ps{gb_}"], writes=[f"fb{tg}"])
                        if n == 0:
                            brs["A"] = ws.next(l, base + 3)
                            brs["B"] = ws.next(l, base + 4)
                        if P.dry:
                            return
                        bsl, bk = (brs["A"] if n < 2 else brs["B"])
                        joff = (n % 2) * 512

                        def mmz(e, n=n, zb=zb, bsl=bsl, joff=joff):
                            for kc in range(4):
                                ins = e.matmul(ps[zb][:], lhsT=bsl[:, joff + kc * 128:joff + (kc + 1) * 128], rhs=y_T[n][:, kc, :],
                                               start=(kc == 0), stop=(kc == 3))
                            return ins
                        P.emit("pe", mmz, reads=[bk, f"y{n}"], writes=[f"ps{zb}"])
                        if n == 0:
                            P.emit("dve", lambda e, tg=tg, zb=zb, acc_i=acc_i: e.scalar_tensor_tensor(
                                out=fb[acc_i][:], in0=fb[tg][:], scalar=1.0, in1=ps[zb][:], op0=ALU.add, op1=ALU.mult),
                                reads=[f"fb{tg}", f"ps{zb}"], writes=[f"fb{acc_i}"])
                        else:
                            P.emit("dve", lambda e, tg=tg, zb=zb: e.scalar_tensor_tensor(
                                out=fb[tg][:], in0=fb[tg][:], scalar=1.0, in1=ps[zb][:], op0=ALU.add, op1=ALU.mult),
                                reads=[f"fb{tg}", f"ps{zb}"], writes=[f"fb{tg}"])
                            if n == 1:
                                P.emit("pool", lambda e, tg=tg, acc_i=acc_i: e.tensor_tensor(
                                    out=fb[acc_i][:], in0=fb[acc_i][:], in1=fb[tg][:], op=ALU.add),
                                    reads=[f"fb{tg}", f"fb{acc_i}"], writes=[f"fb{acc_i}"])
                            else:
                                P.emit("pool", lambda e, tg=tg, acc_i=acc_i, dc=dc: e.tensor_tensor(
                                    out=merged_T[:, dc, :], in0=fb[acc_i][:], in1=fb[tg][:], op=ALU.add),
                                    reads=[f"fb{tg}", f"fb{acc_i}"], writes=["Pn0", "Pn1", "Pn2", "Pn3", "merged"])
                for dc in range(8):
                    merge_dc(dc)
                    if g < 3 and not P.dry:
                        if dc == 0:
                            load_rope(g + 1)
                            p2_prep_q()
                            norm_A(src, sname, s, g + 1, 0, 0)
                            norm_A(src, sname, s, g + 1, 1, 1)
                        elif dc == 2:
                            norm_A(src, sname, s, g + 1, 2, 2)
                        elif dc == 4:
                            norm_A(src, sname, s, g + 1, 3, 3)
                def wout_kc(kc):
                    rb, rk = ws.next(l, P2_BASE + 49 + kc)
                    if P.dry:
                        return

                    def mmo(e):
                        for tt in range(4):
                            for half in range(2):
                                ins = e.matmul(ps[tt * 2 + half][:], lhsT=merged_T[:, kc, tt * 128:(tt + 1) * 128],
                                               rhs=rb[:, half * 512:(half + 1) * 512], start=(kc == 0), stop=(kc == 7))
                        return ins
                    P.emit("pe", mmo, reads=["merged", "Pn0", "Pn1", "Pn2", "Pn3", rk], writes=[f"ps{i}" for i in range(8)])
                if g < 3 and not P.dry:
                    for tt_ in range(4):
                        norm_B(tt_, V_GPM, 0, None, ("hT",), tt_)
                if not P.dry:
                    for tt0 in range(2):
                        tile0 = g * 4 + tt0
                        P.emit("sp", lambda e, tt0=tt0, tile0=tile0: e.dma_start(out=xtb[tt0][:], in_=src[s, tile0 * 128:(tile0 + 1) * 128, :]),
                               reads=[xkey(sname, s, tile0)], writes=[f"xt{tt0}"], dma=True)
                for kc in range(8):
                    wout_kc(kc)

                def wout_tile(tt):
                    if True:
                        tile = g * 4 + tt
                        pbs = [tt * 2, tt * 2 + 1]
                        xt = xtb[tt % 2]
                        kx = f"xt{tt % 2}"
                        if tt >= 2:
                            P.emit("sp", lambda e, xt=xt, tile=tile: e.dma_start(out=xt[:], in_=src[s, tile * 128:(tile + 1) * 128, :]),
                                   reads=[xkey(sname, s, tile)], writes=[kx], dma=True)
                        for half in range(2):
                            pb = pbs[half]

                            P.emit("act", lambda e, pb=pb, half=half: e.activation(
                                out=junk[:, 0:512], in_=ps[pb][:], func=AF.Square, accum_out=stat[:, 4 + half:5 + half]),
                                reads=[f"ps{pb}"], writes=["junk", f"pss{half}"])
                            P.emit("dve", lambda e, pb=pb, half=half: e.tensor_tensor(
                                out=tmpn[:, half * 512:(half + 1) * 512], in0=ps[pb][:], in1=gbp[:, 0, half * 512:(half + 1) * 512],
                                op=ALU.mult), reads=[f"ps{pb}"], writes=["tmpn"])
                        P.emit("dve", lambda e: e.tensor_tensor(out=stat[:, 6:7], in0=stat[:, 4:5], in1=stat[:, 5:6], op=ALU.add),
                               reads=["pss0", "pss1"], writes=["pss"])
                        rms_rstd(stat[:, 6:7], 0.25 / D, stat[:, 7:8], ["pss"], "prs")
                        P.emit("dve", lambda e: e.tensor_scalar(out=stat[:, 7:8], in0=stat[:, 7:8], scalar1=0.5, scalar2=None,
                                                                op0=ALU.mult), reads=["prs"], writes=["prs"])
                        P.emit("dve", lambda e, xt=xt: e.scalar_tensor_tensor(out=xt[:], in0=tmpn[:], scalar=stat[:, 7:8], in1=xt[:],
                                                                               op0=ALU.mult, op1=ALU.add),
                               reads=["tmpn", "prs", kx], writes=[kx])
                        P.emit("sp", lambda e, xt=xt, tile=tile: e.dma_start(out=dst[s, tile * 128:(tile + 1) * 128, :], in_=xt[:]),
                               reads=[kx], writes=[xkey(dname, s, tile)], dma=True)
                if not P.dry:
                    for tt in range(4):
                        wout_tile(tt)

        hrot = Rot((4, 5, 6))
        hcrot = Rot(range(8))

        def phase3(l, s, src, sname, dst, dname, final):
            for g in range(4):
                phase3_group(l, s, src, sname, dst, dname, g)

        def halo_A(s, src, sname, g):
            P.emit("pool", lambda e: e.memset(xh[:], 0.0), writes=["xh"])
            if g > 0:
                tl = g * 512 - 1
                P.emit("sp", lambda e: e.dma_start(out=xh[0:1, :], in_=src[s, tl:tl + 1, :]),
                       reads=[xkey(sname, s, tl // 128)], writes=["xh"], dma=True)
            if g < 3:
                tr_ = (g + 1) * 512
                P.emit("sp", lambda e: e.dma_start(out=xh[1:2, :], in_=src[s, tr_:tr_ + 1, :]),
                       reads=[xkey(sname, s, tr_ // 128)], writes=["xh"], dma=True)
            P.emit("act", lambda e: e.activation(out=junk[0:2, :], in_=xh[:], func=AF.Square, accum_out=stat[0:2, 8:9]),
                   reads=["xh"], writes=["junk", "hss"])
            rms_rstd(stat[0:2, 8:9], 1.0 / D, stat[0:2, 9:10], ["hss"], "hrs")
            P.emit("dve", lambda e: e.tensor_scalar(out=xnh[:], in0=xh[:], scalar1=stat[0:2, 9:10], scalar2=None, op0=ALU.mult),
                   reads=["xh", "hrs"], writes=["xnh"])

        def halo_B():
            def trh(e):
                for c in range(8):
                    ins = e.transpose(out=psT[:, c * 2:(c + 1) * 2], in_=xnh[0:2, c * 128:(c + 1) * 128], identity=ident[0:2, 0:2])
                return ins
            P.emit("pe", trh, reads=["xnh"], writes=["ps7"])
            P.emit("dve", lambda e: e.tensor_tensor(
                out=hT[:, :, 0:514:513], in0=psT[:, 0:16].rearrange("p (c t) -> p c t", c=8),
                in1=vecs[:, V_GPF:V_GPF + 8].unsqueeze(2).to_broadcast([128, 8, 2]), op=ALU.mult),
                reads=["ps7"], writes=["hT"])

        fb3 = list(fb) + [diff_T[:, i // 2, (i % 2) * 1024:(i % 2 + 1) * 1024].bitcast(F32) for i in range(8)]
        fb3rot = Rot(range(16))
        xr = [KC_T[:, i, :].bitcast(F32) for i in range(4)]

        def phase3_group(l, s, src, sname, dst, dname, g):
            if True:
                if g == 0:
                    norm_group(src, sname, s, g, V_GPF, 1)
                    halo_A(s, src, sname, g)
                    halo_B()
                urot = Rot((0, 1, 2, 3))

                def up_chunk(i, which, obuf):
                    if True:
                        col = which * 22 + i
                        rb, rk = ws.next(l, P3_BASE + 2 * i + which)
                        if P.dry:
                            return
                        pb = urot.next()
                        hi_ = hrot.next()
                        w = rb.rearrange("p (kc e) -> p kc e", kc=8)
                        psh = ps[hi_][:, 0:2]

                        def mmu(e, w=w, pb=pb, psh=psh):
                            for kc in range(8):
                                e.matmul(ps[pb][:], lhsT=w[:, kc, :], rhs=hT[:, kc, 1:513], start=(kc == 0), stop=(kc == 7))
                            for kc in range(8):
                                ins = e.matmul(psh, lhsT=w[:, kc, :], rhs=hT[:, kc, 0:514:513], start=(kc == 0), stop=(kc == 7))
                            return ins
                        P.emit("pe", mmu, reads=[rk, "hT"], writes=[f"ps{pb}", f"ps{hi_}"])
                        hc = hcrot.next()
                        hsb = halo_sb[:, hc * 2:hc * 2 + 2]
                        P.emit("act", lambda e, hsb=hsb, psh=psh: e.activation(out=hsb, in_=psh, func=AF.Copy), reads=[f"ps{hi_}"],
                               writes=[f"hsb{hc}"])
                        oi = fb3rot.next()
                        o2i = fb3rot.next()
                        O = fb3[oi]
                        O2 = fb3[o2i]
                        ok = f"fb{oi}"
                        ok2 = f"fb{o2i}"
                        w0 = vecs[:, V_CW0 + col:V_CW0 + col + 1]
                        w1 = vecs[:, V_CW1 + col:V_CW1 + col + 1]
                        w2 = vecs[:, V_CW2 + col:V_CW2 + col + 1]
                        cb = vecs[:, V_CB + col:V_CB + col + 1]
                        P.emit("act", lambda e, O=O, pb=pb, w1=w1, cb=cb: e.activation(out=O[:], in_=ps[pb][:], func=AF.Identity,
                                                                                        bias=cb, scale=w1),
                               reads=[f"ps{pb}"], writes=[ok])
                        P.emit("act", lambda e, O2=O2, pb=pb, w0=w0: e.activation(out=O2[:], in_=ps[pb][:], func=AF.Copy, scale=w0),
                               reads=[f"ps{pb}"], writes=[ok2])
                        P.emit("dve", lambda e, O=O, pb=pb, w2=w2: e.scalar_tensor_tensor(
                            out=O[:, 0:511], in0=ps[pb][:, 1:512], scalar=w2, in1=O[:, 0:511], op0=ALU.mult, op1=ALU.add),
                            reads=[f"ps{pb}", ok], writes=[ok])
                        P.emit("dve", lambda e, O=O, hsb=hsb, w2=w2: e.scalar_tensor_tensor(
                            out=O[:, 511:512], in0=hsb[:, 1:2], scalar=w2, in1=O[:, 511:512], op0=ALU.mult, op1=ALU.add),
                            reads=[f"hsb{hc}", ok], writes=[ok])
                        P.emit("dve", lambda e, O=O, hsb=hsb, w0=w0: e.scalar_tensor_tensor(
                            out=O[:, 0:1], in0=hsb[:, 0:1], scalar=w0, in1=O[:, 0:1], op0=ALU.mult, op1=ALU.add),
                            reads=[f"hsb{hc}", ok], writes=[ok])
                        P.emit("pool", lambda e, O=O, O2=O2: e.tensor_tensor(out=O[:, 1:512], in0=O[:, 1:512], in1=O2[:, 0:511],
                                                                              op=ALU.add), reads=[ok, ok2], writes=[ok])
                        obuf.append((O, ok))

                def up_pair(i):
                    obuf = []
                    for which in range(2):
                        up_chunk(i, which, obuf)
                    return obuf

                def up_finish(i, obuf):
                    (Ov, okv), (Og, okg) = obuf
                    P.emit("act", lambda e, Og=Og: e.activation(out=Og[:], in_=Og[:], func=AF.Gelu_apprx_tanh), reads=[okg], writes=[okg])
                    P.emit("dve", lambda e, Ov=Ov, Og=Og, i=i: e.tensor_tensor(out=act_T[:, i, :], in0=Og[:], in1=Ov[:], op=ALU.mult),
                           reads=[okv, okg], writes=["act_T"])
                prev = None
                for i in range(22):
                    cur_ = up_pair(i)
                    if prev is not None and not P.dry:
                        up_finish(i - 1, prev)
                    prev = cur_
                if not P.dry:
                    up_finish(21, prev)
                if not P.dry:
                    for tt0 in range(4):
                        tile0 = g * 4 + tt0
                        P.emit("sp", lambda e, tt0=tt0, tile0=tile0: e.dma_start(out=xr[tt0], in_=src[s, tile0 * 128:(tile0 + 1) * 128, :]),
                               reads=[xkey(sname, s, tile0)], writes=[f"xr{tt0}"], dma=True)
                sched = {}
                if g < 3 and not P.dry:
                    gn = g + 1
                    norm_A(src, sname, s, gn, 0)
                    norm_A(src, sname, s, gn, 1)
                    sched = {3: [lambda: norm_B(0, V_GPF, 1), lambda: norm_A(src, sname, s, gn, 2)],
                             6: [lambda: norm_B(1, V_GPF, 1), lambda: norm_A(src, sname, s, gn, 3)],
                             9: [lambda: norm_B(2, V_GPF, 1)],
                             13: [lambda: norm_B(3, V_GPF, 1), lambda: halo_A(s, src, sname, gn)],
                             17: [lambda: halo_B()]}
                nop = 0
                for half in range(2):
                    for j2 in range(11):
                        rb, rk = ws.next(l, P3_BASE + 44 + half * 11 + j2)
                        if P.dry:
                            continue

                        def mmd(e, rb=rb, j2=j2):
                            for j in range(2):
                                i = 2 * j2 + j
                                for tt in range(4):
                                    ins = e.matmul(ps[tt][:], lhsT=act_T[:, i, tt * 128:(tt + 1) * 128], rhs=rb[:, j * 512:(j + 1) * 512],
                                                   start=(i == 0), stop=(i == 21))
                            return ins
                        P.emit("pe", mmd, reads=[rk, "act_T"], writes=["ps0", "ps1", "ps2", "ps3"])
                        for f_ in sched.get(nop, ()):
                            f_()
                        nop += 1
                    if P.dry:
                        continue
                    for tt in range(4):
                        P.emit("act", lambda e, tt=tt, half=half: e.activation(
                            out=junk[:, 0:512], in_=ps[tt][:], func=AF.Square, accum_out=stat[:, 10 + tt * 2 + half:11 + tt * 2 + half]),
                            reads=[f"ps{tt}"], writes=["junk", f"dss{tt}_{half}"])
                        P.emit("dve", lambda e, tt=tt, half=half: e.tensor_tensor(
                            out=f_sb[tt][:, half * 512:(half + 1) * 512], in0=ps[tt][:], in1=gbp[:, 1, half * 512:(half + 1) * 512],
                            op=ALU.mult), reads=[f"ps{tt}"], writes=[f"f_sb{tt}"])
                if P.dry:
                    return

                def tail_tile(tt):
                    tile = g * 4 + tt
                    a = 10 + tt * 2
                    P.emit("dve", lambda e: e.tensor_tensor(out=stat[:, 6:7], in0=stat[:, a:a + 1], in1=stat[:, a + 1:a + 2],
                                                            op=ALU.add), reads=[f"dss{tt}_0", f"dss{tt}_1"], writes=["pss"])
                    rms_rstd(stat[:, 6:7], 1.0 / D, stat[:, 7:8], ["pss"], "prs")
                    P.emit("dve", lambda e: e.scalar_tensor_tensor(out=f_sb[tt][:], in0=f_sb[tt][:], scalar=stat[:, 7:8],
                                                                   in1=xr[tt], op0=ALU.mult, op1=ALU.add),
                           reads=["prs", f"xr{tt}"], writes=[f"f_sb{tt}"])
                    P.emit("sp", lambda e: e.dma_start(out=dst[s, tile * 128:(tile + 1) * 128, :], in_=f_sb[tt][:]),
                           reads=[f"f_sb{tt}"], writes=[xkey(dname, s, tile)], dma=True)
                for tt in range(4):
                    tail_tile(tt)

        def body():
            if not P.dry:
                load_consts()
            for l in range(L):
                if not P.dry:
                    load_layer_consts(l)
                src, sname = (x_in, "in") if l == 0 else (xsB, "B")
                last = (l == L - 1)
                dstf, dfname = (out, "out") if last else (xsB, "B")
                for s in range(NSEQ):
                    if 1 in phases:
                        phase1(l, s, src, sname)
                        P.barrier()
                    if 2 in phases:
                        phase2(l, s, src, sname, xsA, "A")
                        P.barrier()
                    if 3 in phases:
                        phase3(l, s, xsA if 2 in phases else src, "A", dstf, dfname, last)
                        P.barrier()

        P.dry = True
        body()
        P.dry = False
        body()
        P.finish(block, sems)
        if os.environ.get("KDBG_LOG"):
            with open(os.environ["KDBG_LOG"], "w") as fh:
                for r in P.log:
                    fh.write("%d %s %d\n" % r)
    return nc


_CACHE = {}


def kernel(x, norm_mix_pre, norm_mix_post, norm_ffn_pre, norm_ffn_post, w_in, b_gate, qk_norm_q, qk_norm_k,
           w_pool, pool_scale, rpb, w_branch, w_out, w_up, conv_w, conv_b, w_down):
    f = lambda a: np.ascontiguousarray(np.asarray(a, dtype=np.float32))
    x = f(x)
    L = 2
    wslots = np.stack([build_wslots(f(w_in[l]), f(w_pool[l]), f(w_branch[l]), f(w_out[l]), f(w_up[l]), f(w_down[l]))
                       for l in range(L)])
    vecs = np.stack([build_vecs(f(norm_mix_pre[l]), f(norm_ffn_pre[l]), f(b_gate[l]), f(conv_w[l]), f(conv_b[l]),
                                f(pool_scale[l]), f(qk_norm_q[l]), f(qk_norm_k[l])) for l in range(L)])
    gbp = np.stack([np.stack([f(norm_mix_post[l]), f(norm_ffn_post[l])]) for l in range(L)])
    natab = np.stack([build_natab(f(rpb[l])) for l in range(L)])
    cst, rope = build_consts()
    if "nc" not in _CACHE:
        _CACHE["nc"] = build_nc(L=2, NSEQ=2)
    nc = _CACHE["nc"]
    n = 8
    in_maps = []
    for c in range(n):
        in_maps.append({"x": np.ascontiguousarray(x[2 * c:2 * c + 2]), "wslots": wslots, "vecs": vecs, "gbp": gbp,
                        "natab": natab, "cst": cst, "rope": rope})
    res = run_bass_kernel_spmd(nc, in_maps, core_ids=list(range(n)))
    return np.concatenate([r["out"] for r in res.results], axis=0)
```
